# Optimizing a Trainium2 kernel written in Bass

```python
import jax, jax.numpy as jnp
from jax import lax
import numpy as np

D_MODEL = 1024
BATCH = 32
SEQ = 256
DEPTH = 4
DEC_BATCH = 4
DEC_SEQ = 4096
PAST_LEN = 256

GRID_W = 64
MIX_WIDTH = D_MODEL
GROUP_W = MIX_WIDTH // 4
A_GROUPS = 4
A_CH = GROUP_W // A_GROUPS
CHUNK = 128
B_WIDTH = GROUP_W
CONV_WIDTH = 3
C_GROUPS = 4
C_CH = GROUP_W // C_GROUPS
MLA_HEADS = 4
QK_NOPE = 64
QK_ROPE = 32
V_DIM = 64
Q_LORA = 192
KV_LORA = 128
ROPE_BASE = 10000.0
AXIS_ROPE = QK_ROPE // 2
Q_BLOCK = 128
FF_HIDDEN = ((8 * D_MODEL // 3 + 255) // 256) * 256
EPS = 1e-6
P_A = 2 * GROUP_W
P_B = 3 * B_WIDTH
P_C = GROUP_W
P_D = Q_LORA + KV_LORA + QK_ROPE
P_TOTAL = P_A + P_B + P_C + P_D
SPLITS = [GROUP_W, 2 * GROUP_W, 2 * GROUP_W + B_WIDTH, 2 * GROUP_W + 2 * B_WIDTH,
          2 * GROUP_W + 3 * B_WIDTH, 2 * GROUP_W + 3 * B_WIDTH + P_C]

kernel_name = 'hybrid_diffusion_prefix_trunk'


def rmsnorm(x, g):
    xf = x.astype(jnp.float32)
    y = xf * lax.rsqrt(jnp.mean(xf * xf, axis=-1, keepdims=True) + EPS)
    return (y * g.astype(jnp.float32)).astype(x.dtype)


def modulation(cond, w_ada, b_ada):
    m = jax.nn.silu(cond) @ w_ada + b_ada
    if cond.ndim == 2:
        m = m[:, None, :]
    return jnp.split(m, 6, axis=-1)


def rope_2d_tables(n):
    rows = n // GRID_W
    row = jnp.repeat(jnp.arange(rows, dtype=jnp.float32), GRID_W)
    col = jnp.tile(jnp.arange(GRID_W, dtype=jnp.float32), rows)
    inv = ROPE_BASE ** (-jnp.arange(0, AXIS_ROPE, 2, dtype=jnp.float32) / AXIS_ROPE)
    ang = jnp.stack([row[:, None] * inv, col[:, None] * inv], axis=1)
    return jnp.cos(ang), jnp.sin(ang)


def apply_rope_2d(x, cos, sin):
    xf = x.astype(jnp.float32)
    xr = xf.reshape(*x.shape[:-1], 2, 2, AXIS_ROPE // 2)
    x1, x2 = xr[..., 0, :], xr[..., 1, :]
    extra = x.ndim - 3
    c = cos.reshape(cos.shape[0], *([1] * extra), 2, AXIS_ROPE // 2)
    s = sin.reshape(sin.shape[0], *([1] * extra), 2, AXIS_ROPE // 2)
    out = jnp.stack([x1 * c - x2 * s, x2 * c + x1 * s], axis=-2)
    return out.reshape(x.shape).astype(x.dtype)


def chunk_mlp(u, v, spat_w, spat_b):
    b, n, _ = v.shape
    vr = v.reshape(b, n // CHUNK, CHUNK, A_GROUPS, A_CH)
    mixed = jnp.einsum('gpq,bnqgc->bnpgc', spat_w, vr) + spat_b.T[None, None, :, :, None]
    return u * mixed.reshape(b, n, GROUP_W)


def short_conv(h, gate_b, gate_c, conv_w, conv_b):
    z = gate_c * h
    zp = jnp.pad(z, ((0, 0), (1, 1), (0, 0)))
    y = zp[:, :-2] * conv_w[0] + zp[:, 1:-1] * conv_w[1] + zp[:, 2:] * conv_w[2] + conv_b
    return gate_b * y


def fourier_mix(x):
    b, n, _ = x.shape
    xg = x.astype(jnp.float32).reshape(b, n, C_GROUPS, C_CH)
    y = jnp.fft.fft2(xg, axes=(1, 3), norm='ortho').real
    return y.reshape(b, n, GROUP_W).astype(x.dtype)


def mla_project(pd, g_q_lora, w_uq, g_kv_lora):
    b, n, _ = pd.shape
    cq, ckv, k_rope = jnp.split(pd, [Q_LORA, Q_LORA + KV_LORA], axis=-1)
    q = (rmsnorm(cq, g_q_lora) @ w_uq).reshape(b, n, MLA_HEADS, QK_NOPE + QK_ROPE)
    q_nope, q_rope = q[..., :QK_NOPE], q[..., QK_NOPE:]
    return q_nope, q_rope, rmsnorm(ckv, g_kv_lora), k_rope


def mla_expand(ckv, w_ukv):
    b, n, _ = ckv.shape
    kv = (ckv @ w_ukv).reshape(b, n, MLA_HEADS, QK_NOPE + V_DIM)
    return kv[..., :QK_NOPE], kv[..., QK_NOPE:]


def mla_attend(q_nope, q_rope, k_nope, k_rope, v):
    b, n = q_nope.shape[:2]
    nb = n // Q_BLOCK
    qn_b = q_nope.reshape(b, nb, Q_BLOCK, MLA_HEADS, QK_NOPE).swapaxes(0, 1)
    qr_b = q_rope.reshape(b, nb, Q_BLOCK, MLA_HEADS, QK_ROPE).swapaxes(0, 1)
    scale = (QK_NOPE + QK_ROPE) ** -0.5

    def block(args):
        qn_i, qr_i = args
        s = (jnp.einsum('bqhd,bkhd->bhqk', qn_i, k_nope, preferred_element_type=jnp.float32)
             + jnp.einsum('bqhr,bkr->bhqk', qr_i, k_rope, preferred_element_type=jnp.float32))
        p = jax.nn.softmax(s * scale, axis=-1).astype(v.dtype)
        return jnp.einsum('bhqk,bkhd->bqhd', p, v)

    o = lax.map(block, (qn_b, qr_b))
    return o.swapaxes(0, 1).reshape(b, n, MLA_HEADS * V_DIM)


def swiglu(h, w_gate_up, w_down):
    g, u = jnp.split(h @ w_gate_up, 2, axis=-1)
    return (jax.nn.silu(g) * u) @ w_down


def trunk_layer(x, cond, ctx_ckv, ctx_krope, w_ada, b_ada, g_pre_mix, g_post_mix,
                g_pre_ffn, g_post_ffn, w_in, spat_w, spat_b, conv_w, conv_b,
                g_q_lora, w_uq, g_kv_lora, w_ukv, w_out, w_gate_up, w_down):
    shift1, scale1, gate1, shift2, scale2, gate2 = modulation(cond, w_ada, b_ada)
    h = rmsnorm(x, g_pre_mix) * (1.0 + scale1) + shift1
    pa_u, pa_v, pb_h, pb_b, pb_c, pc, pd = jnp.split(h @ w_in, SPLITS, axis=-1)
    y_a = chunk_mlp(pa_u, pa_v, spat_w, spat_b)
    y_b = short_conv(pb_h, pb_b, pb_c, conv_w, conv_b)
    y_c = fourier_mix(pc)
    q_nope, q_rope, ckv, k_rope = mla_project(pd, g_q_lora, w_uq, g_kv_lora)
    if ctx_ckv is None:
        k_nope, v = mla_expand(ckv, w_ukv)
        y_d = mla_attend(q_nope, q_rope, k_nope, k_rope, v)
    else:
        cos, sin = rope_2d_tables(x.shape[1])
        q_rope = apply_rope_2d(q_rope, cos, sin)
        k_rope = apply_rope_2d(k_rope, cos, sin)
        k_nope, v = mla_expand(jnp.concatenate([ckv, ctx_ckv], axis=1), w_ukv)
        k_rope_all = jnp.concatenate([k_rope, ctx_krope], axis=1)
        y_d = mla_attend(q_nope, q_rope, k_nope, k_rope_all, v)
    mix = jnp.concatenate([y_a, y_b, y_c, y_d], axis=-1) @ w_out
    x = x + gate1 * rmsnorm(mix, g_post_mix)
    h2 = rmsnorm(x, g_pre_ffn) * (1.0 + scale2) + shift2
    x = x + gate2 * rmsnorm(swiglu(h2, w_gate_up, w_down), g_post_ffn)
    return x, ckv, k_rope


def setup_inputs(seed: int = 0) -> dict:
    key = jax.random.key(seed)
    ks = jax.random.split(key, 24)

    def nrm(k, shape, s):
        return jax.random.normal(k, shape, jnp.float32) * s

    return {
        'x_prompt': nrm(ks[0], (BATCH, SEQ, D_MODEL), 1.0),
        'x_sample': nrm(ks[1], (DEC_BATCH, DEC_SEQ, D_MODEL), 1.0),
        'cache_ckv': nrm(ks[2], (DEC_BATCH, DEPTH, PAST_LEN, KV_LORA), 1.0),
        'cache_krope': nrm(ks[3], (DEC_BATCH, DEPTH, PAST_LEN, QK_ROPE), 1.0),
        'c': nrm(ks[4], (DEC_BATCH, D_MODEL), 1.0),
        'c_ctx': nrm(ks[5], (D_MODEL,), 1.0),
        'w_ada': nrm(ks[6], (DEPTH, D_MODEL, 6 * D_MODEL), 0.5 * D_MODEL ** -0.5),
        'b_ada': nrm(ks[7], (DEPTH, 6 * D_MODEL), 0.01),
        'g_pre_mix': 1.0 + nrm(ks[8], (DEPTH, D_MODEL), 0.01),
        'g_post_mix': 1.0 + nrm(ks[9], (DEPTH, D_MODEL), 0.01),
        'g_pre_ffn': 1.0 + nrm(ks[10], (DEPTH, D_MODEL), 0.01),
        'g_post_ffn': 1.0 + nrm(ks[11], (DEPTH, D_MODEL), 0.01),
        'w_in': nrm(ks[12], (DEPTH, D_MODEL, P_TOTAL), D_MODEL ** -0.5),
        'spat_w': nrm(ks[13], (DEPTH, A_GROUPS, CHUNK, CHUNK), CHUNK ** -0.5),
        'spat_b': 1.0 + nrm(ks[14], (DEPTH, A_GROUPS, CHUNK), 0.01),
        'conv_w': nrm(ks[15], (DEPTH, CONV_WIDTH, B_WIDTH), CONV_WIDTH ** -0.5),
        'conv_b': nrm(ks[16], (DEPTH, B_WIDTH), 0.01),
        'g_q_lora': 1.0 + nrm(ks[17], (DEPTH, Q_LORA), 0.01),
        'w_uq': nrm(ks[18], (DEPTH, Q_LORA, MLA_HEADS * (QK_NOPE + QK_ROPE)), Q_LORA ** -0.5),
        'g_kv_lora': 1.0 + nrm(ks[19], (DEPTH, KV_LORA), 0.01),
        'w_ukv': nrm(ks[20], (DEPTH, KV_LORA, MLA_HEADS * (QK_NOPE + V_DIM)), KV_LORA ** -0.5),
        'w_out': nrm(ks[21], (DEPTH, MIX_WIDTH, D_MODEL), MIX_WIDTH ** -0.5),
        'w_gate_up': nrm(ks[22], (DEPTH, D_MODEL, 2 * FF_HIDDEN), D_MODEL ** -0.5),
        'w_down': nrm(ks[23], (DEPTH, FF_HIDDEN, D_MODEL), FF_HIDDEN ** -0.5),
    }


def reference(x_prompt, x_sample, cache_ckv, cache_krope, c, c_ctx, w_ada, b_ada,
              g_pre_mix, g_post_mix, g_pre_ffn, g_post_ffn, w_in, spat_w, spat_b,
              conv_w, conv_b, g_q_lora, w_uq, g_kv_lora, w_ukv, w_out, w_gate_up, w_down):
    y_prompt = x_prompt
    y_sample = x_sample
    ckv_list = []
    krope_list = []
    for l in range(DEPTH):
        lp = (w_ada[l], b_ada[l], g_pre_mix[l], g_post_mix[l], g_pre_ffn[l], g_post_ffn[l],
              w_in[l], spat_w[l], spat_b[l], conv_w[l], conv_b[l], g_q_lora[l], w_uq[l],
              g_kv_lora[l], w_ukv[l], w_out[l], w_gate_up[l], w_down[l])
        y_prompt, ckv_l, krope_l = trunk_layer(y_prompt, c_ctx, None, None, *lp)
        ckv_list.append(ckv_l)
        krope_list.append(krope_l)
        y_sample, _, _ = trunk_layer(y_sample, c, cache_ckv[:, l], cache_krope[:, l], *lp)
    state_ckv = jnp.stack(ckv_list, axis=1)
    state_krope = jnp.stack(krope_list, axis=1)
    return (y_prompt, y_sample, state_ckv, state_krope)
```

```python
import numpy as np
import ml_dtypes
from contextlib import ExitStack
import concourse.bass as bass
import concourse.mybir as mybir
from concourse.bass_utils import run_bass_kernel_spmd

F32 = mybir.dt.float32
BF16 = mybir.dt.bfloat16
AF = mybir.ActivationFunctionType
ALU = mybir.AluOpType

DEPTH = 4
NSP = 91
TP, TS = 1024, 2048
EPS = 1e-6
ATT_SCALE = float((64 + 32) ** -0.5)
NSLOT = 14


class Cell:
    __slots__ = ("w", "r")

    def __init__(s):
        s.w = None
        s.r = {}


class Root:
    def __init__(s, cs):
        s.cs = cs
        s.cells = {}

    def get(s, lo, hi):
        out = []
        for i in range(lo // s.cs, (hi - 1) // s.cs + 1):
            c = s.cells.get(i)
            if c is None:
                c = s.cells[i] = Cell()
            out.append(c)
        return out


class View:
    __slots__ = ("ap", "root", "lo", "hi")

    def __init__(s, ap, root, lo, hi):
        s.ap, s.root, s.lo, s.hi = ap, root, lo, hi

    def cells(s):
        return s.root.get(s.lo, s.hi)


class Tile:
    def __init__(s, ap, shape, es, root=None, boff=0, cs=512):
        s.ap, s.shape, s.es = ap, list(shape), es
        s.root = root if root is not None else Root(cs)
        s.boff = boff
        st = [1] * len(shape)
        for i in range(len(shape) - 2, 0, -1):
            st[i] = st[i + 1] * shape[i + 1]
        s.st = st

    def __getitem__(s, idx):
        if not isinstance(idx, tuple):
            idx = (idx,)
        idx = tuple(idx) + (slice(None),) * (len(s.shape) - len(idx))
        lo = 0
        hi = 0
        for d in range(1, len(s.shape)):
            ix = idx[d]
            if isinstance(ix, int):
                a, b = ix, ix + 1
            else:
                a = 0 if ix.start is None else ix.start
                b = s.shape[d] if ix.stop is None else ix.stop
            assert 0 <= a < b <= s.shape[d], (idx, s.shape)
            lo += a * s.st[d]
            hi += (b - 1) * s.st[d]
        return View(s.ap[idx], s.root, s.boff + lo * s.es, s.boff + (hi + 1) * s.es)

    def all(s):
        return s[tuple(slice(None) for _ in s.shape)]


class DTile:
    def __init__(s, ap):
        s.ap = ap
        s.root = Root(1 << 40)

    def v(s, ap=None):
        return View(s.ap if ap is None else ap, s.root, 0, 1)

    def fresh(s, ap):
        return View(ap, Root(1 << 40), 0, 1)


class Eng:
    def __init__(s, name, is_pe=False):
        s.name = name
        s.is_pe = is_pe
        s.ops = []
        s.count = 0
        s.waited = {}
        s.slots = []
        s.slot_i = 0


class Prog:
    def __init__(s):
        s.pe = Eng("pe", True)
        s.act = Eng("act")
        s.dve = Eng("dve")
        s.pool = Eng("pool")
        s.sp = Eng("sp")
        s.cc_count = 0
        for q in (s.sp, s.pool):
            q.slots = [[f"{q.name}_d{i}", 0] for i in range(NSLOT)]

    def emit(s, eng, fn, reads, writes, kind="op", final=True):
        deps = {}

        def add(sig):
            if sig is not None and deps.get(sig[0], 0) < sig[1]:
                deps[sig[0]] = sig[1]

        rc = [c for v in reads for c in v.cells()]
        wc = [c for v in writes for c in v.cells()]
        for c in rc:
            add(c.w)
        for c in wc:
            add(c.w)
            for k, val in c.r.items():
                add((k, val))
        waits = []
        for k, v in deps.items():
            if eng.is_pe and k == eng.name:
                continue
            if eng.waited.get(k, 0) >= v:
                continue
            eng.waited[k] = v
            waits.append((k, v))
        if kind == "dma":
            slot = eng.slots[eng.slot_i]
            eng.slot_i = (eng.slot_i + 1) % len(eng.slots)
            prev = 16 * slot[1]
            if prev > 0 and eng.waited.get(slot[0], 0) < prev:
                eng.waited[slot[0]] = prev
                waits.append((slot[0], prev))
            slot[1] += 1
            sig = (slot[0], 16 * slot[1])
            inc = (slot[0], 16)
        elif kind == "cc":
            s.cc_count += 1
            sig = ("cc", s.cc_count)
            inc = ("cc", None)
        else:
            sig = (eng.name, eng.count + 1)
            if final:
                eng.count += 1
                inc = (eng.name, 1)
            else:
                inc = None
        eng.ops.append((waits, fn, inc))
        for c in rc:
            if c.r.get(sig[0], 0) < sig[1]:
                c.r[sig[0]] = sig[1]
        for c in wc:
            c.w = sig
            c.r = {}
        return sig


def build_program(depth=DEPTH, groups=None, stages="mod,SA,EX,P,SM,SF"):
    groups = groups or [[0, 1], [2, 3], [4, 5], [6, 7]]
    stages = set(stages.split(","))
    plevel = 9
    for t_ in list(stages):
        if t_.startswith("P:"):
            plevel = float(t_[2:])
            stages.add("P")
    nc = bass.Bass("TRN2", target_bir_lowering=False)
    P = Prog()

    def din(name, shape, dt=F32):
        return nc.dram_tensor(name, list(shape), dt, kind="ExternalInput").ap()

    def dout(name, shape, dt=F32):
        return nc.dram_tensor(name, list(shape), dt, kind="ExternalOutput").ap()

    def dscr(name, shape, dt=BF16):
        return nc.dram_tensor(name, list(shape), dt)

    xT_p = din("xT_p", [8, 128, TP])
    xT_s = din("xT_s", [8, 128, TS])
    condT = din("condT", [128, 8, 2])
    cacheT_ckv = din("cacheT_ckv", [DEPTH, 128, 256])
    cacheT_kr = din("cacheT_kr", [DEPTH, 32, 256])
    w_ada = din("w_ada", [DEPTH, 1024, 6144])
    w_in_p = din("w_in_p", [DEPTH, 1024, 2176])
    w_out = din("w_out", [DEPTH, 1024, 1024])
    w_gu_p = din("w_gu_p", [DEPTH, 1024, 5632])
    w_down = din("w_down", [DEPTH, 2816, 1024])
    spatT_d = din("spatT", [DEPTH, 128, 4, 128])
    spb_d = din("spb_bc", [DEPTH, 128, 2, 128])
    w_uq_d = din("w_uq_p", [DEPTH, 128, 2, 2, 512])
    w_ukv_d = din("w_ukv_p", [DEPTH, 128, 768])
    smallp_d = din("smallp", [128, DEPTH * NSP])
    corep_d = din("corep", [128, 2])
    ropeT_d = din("ropeT", [2, 128, TS])
    dft256_d = din("dft256", [128, 2, 2, 256], BF16)
    dft4096_d = din("dft4096", [2, 4096, 2048], BF16)
    dftc_d = din("dftc", [128, 2, 128], BF16)
    shiftm_d = din("shiftm", [128, 128])

    yp_o = dout("yp_o", [8, 128, TP])
    ys_o = dout("ys_o", [8, 128, TS])
    ckv_o = dout("ckv_o", [DEPTH, 128, TP])
    kr_o = dout("kr_o", [DEPTH, 32, TP])

    xs_p = dscr("xs_p", [8, 128, TP], F32)
    xs_s = dscr("xs_s", [8, 128, TS], F32)
    zS = dscr("zS", [2, 128, TS + 2])
    gbS = dscr("gbS", [2, 128, TS])
    QS = dscr("QS", [4, 128, TS])
    mixS = dscr("mixS", [8, 128, TS])
    xk_in = dscr("xk_in", [164, TS])
    xk_out = dscr("xk_out", [328, TS])
    pc_in = dscr("pc_in", [TS, 256])
    pc_out = dscr("pc_out", [2 * TS, 256])
    D_xs_p, D_xs_s = DTile(xs_p.ap()), DTile(xs_s.ap())
    D_zS, D_gbS, D_QS, D_mixS = DTile(zS.ap()), DTile(gbS.ap()), DTile(QS.ap()), DTile(mixS.ap())
    D_xk_in, D_xk_out = DTile(xk_in.ap()), DTile(xk_out.ap())
    D_pc_in, D_pc_out = DTile(pc_in.ap()), DTile(pc_out.ap())
    D_in = DTile(xT_p)
    D_out = DTile(yp_o)

    es_of = {F32: 4, BF16: 2}

    with ExitStack() as st:
        def sb(name, shape, dt, cs=512):
            h = st.enter_context(nc.sbuf_tensor("sb_" + name, list(shape), dt))
            return Tile(h[tuple(slice(None) for _ in shape)], shape, es_of[dt], cs=cs)

        psb = []
        for i in range(8):
            h = st.enter_context(nc.psum_tensor(f"ps{i}", [128, 512], F32))
            psb.append(Tile(h[:, :], [128, 512], 4, cs=4096))

        ones = sb("ones", [128, 128], BF16)
        shiftm = sb("shiftm", [128, 128], F32)
        smallp = sb("smallp", [128, DEPTH * NSP], F32, cs=4)
        corep = sb("corep", [128, 2], F32, cs=4)
        epsT = sb("epsT", [128, 1], F32)
        scond = sb("scond", [128, 8, 2], BF16, cs=4)
        condS = sb("condS", [128, 8, 2], F32)
        modA = sb("modA", [128, DEPTH, 48, 2], F32, cs=8)
        gs1 = sb("gs1", [128, DEPTH, 8, 2], F32, cs=8)
        gg1 = sb("gg1", [128, DEPTH, 8, 2], F32, cs=8)
        gs2 = sb("gs2", [128, DEPTH, 8, 2], F32, cs=8)
        gg2 = sb("gg2", [128, DEPTH, 8, 2], F32, cs=8)
        dft256 = sb("dft256", [128, 2, 2, 256], BF16)
        dftc = sb("dftc", [128, 2, 128], BF16)
        spatT = sb("spatT", [128, 4, 128], BF16)
        spb = sb("spb", [128, 2, 128], F32)
        wuq = sb("wuq", [128, 2, 2, 512], BF16)
        wukv = sb("wukv", [128, 768], BF16)
        f32big = [sb(f"f32big{i}", [128, 8, 512], F32) for i in range(2)]
        sq = sb("sq", [128, 8, 512], BF16)
        sqq = sb("sqq", [128, 3, 512], BF16)
        rstd = [sb(f"rstd{i}", [128, 512], F32) for i in range(2)]
        sqrtT = [sb(f"sqrt{i}", [128, 512], F32) for i in range(1)]
        tmpf = [sb(f"tmpf{i}", [128, 512], F32) for i in range(3)]
        hcp = [sb(f"hcp{i}", [128, 512], F32) for i in range(2)]
        H = sb("H", [128, 8, 512], BF16)
        ublk = sb("ublk", [128, 2, 512], BF16)
        zext = sb("zext", [128, 2, 516], BF16)
        gbblk = sb("gbblk", [128, 2, 512], BF16)
        vptok = sb("vptok", [128, 4, 512], BF16)
        cqn = sb("cqn", [128, 2, 512], BF16)
        ckvf = sb("ckvf", [128, 512], F32)
        ckvb = sb("ckvb", [128, 512], BF16)
        krst = sb("krst", [128, 512], BF16)
        krf = sb("krf", [128, 512], F32)
        Qblk = sb("Qblk", [128, 4, 512], BF16)
        mixblk = sb("mixblk", [128, 8, 512], BF16)
        ymix = [sb(f"ymix{i}", [128, 2, 512], BF16) for i in range(2)]
        hal = sb("hal", [128, 2, 2, 16], BF16, cs=32)
        halm = sb("halm", [128, 2, 2], BF16, cs=2)
        sg = [sb(f"sg{i}", [128, 512], F32) for i in range(2)]
        convy = sg
        Pt = [sb(f"Pt{i}", [128, 512], BF16) for i in range(4)]
        Osb = [sb(f"Osb{i}", [128, 512], F32) for i in range(2)]
        rEO = [sb(f"rEO{i}", [128, 512], F32) for i in range(2)]
        UVsb = sb("UVsb", [128, 4, 512], BF16)
        wbuf = [sb(f"wbuf{i}", [128, 8, 512], BF16) for i in range(2)]
        RB = 51712
        Rh = st.enter_context(nc.sbuf_tensor("Rreg", [128, RB // 2], BF16))
        Rroot = Root(512)

        def carve(off, shape, dt):
            n = int(np.prod(shape[1:])) * es_of[dt]
            assert off % 4 == 0 and off + n <= RB, (off, n)
            ap = Rh[:, off // 2:(off + n) // 2]
            if dt == F32:
                ap = ap.bitcast(F32)
            if len(shape) == 3:
                ap = ap.rearrange("p (a b) -> p a b", a=shape[1])
            elif len(shape) == 4:
                ap = ap.rearrange("p (a b c) -> p a b c", a=shape[1], b=shape[2])
            return Tile(ap, shape, es_of[dt], root=Rroot, boff=off)

        ropeT = carve(0, [128, 2, TS], F32)
        pc_all = carve(0, [128, 32, 256], BF16)
        dftbuf = [carve(16384 + i * 8192, [128, 8, 512], BF16) for i in range(4)]
        ckv_all = carve(0, [128, 4352], BF16)
        K2 = [carve(8704 + i * 8704, [128, 4352], BF16) for i in range(2)]
        V2 = [carve(26112 + i * 8704, [128, 34, 128], BF16) for i in range(2)]
        Qh = [carve(43520 + i * 4096, [128, TS], BF16) for i in range(2)]
        Affn = carve(0, [128, 22, 512], BF16)
        wdbuf = [carve(22528 + i * 11264, [128, 22, 256], BF16) for i in range(2)]

        def apx(x):
            return x.ap if isinstance(x, View) else x

        def rd(*xs):
            return [x for x in xs if isinstance(x, View)]

        def dma(q, out, in_, **kw):
            eng = P.sp if q == "sp" else P.pool
            o, i = out.ap, in_.ap
            P.emit(eng, lambda e: e.dma_start(out=o, in_=i, **kw), [in_], [out], kind="dma")

        def act(out, in_, func, bias=None, scale=1.0):
            o, i, b, sc = out.ap, in_.ap, apx(bias), apx(scale)
            kw = {}
            if b is not None:
                kw["bias"] = b
            P.emit(P.act, lambda e: e.activation(out=o, in_=i, func=func, scale=sc, **kw),
                   [in_] + rd(bias, scale), [out])

        def tt(out, a, b, op, eng="dve"):
            E = P.dve if eng == "dve" else P.pool
            o, x, y = out.ap, a.ap, b.ap
            P.emit(E, lambda e: e.tensor_tensor(out=o, in0=x, in1=y, op=op), [a, b], [out])

        def stt(out, in0, scalar, in1, op0, op1, eng="dve"):
            E = P.dve if eng == "dve" else P.pool
            o, x, s_, y = out.ap, in0.ap, apx(scalar), in1.ap
            P.emit(E, lambda e: e.scalar_tensor_tensor(out=o, in0=x, scalar=s_, in1=y, op0=op0, op1=op1),
                   [in0, in1] + rd(scalar), [out])

        def ts(out, in0, s1, op0, s2=None, op1=None, eng="dve"):
            E = P.dve if eng == "dve" else P.pool
            o, x, a1, a2 = out.ap, in0.ap, apx(s1), apx(s2)
            if op1 is None:
                P.emit(E, lambda e: e.tensor_scalar(out=o, in0=x, scalar1=a1, scalar2=None, op0=op0),
                       [in0] + rd(s1), [out])
            else:
                P.emit(E, lambda e: e.tensor_scalar(out=o, in0=x, scalar1=a1, scalar2=a2, op0=op0, op1=op1),
                       [in0] + rd(s1, s2), [out])

        def cp(out, in_, eng="dve"):
            E = P.dve if eng == "dve" else P.pool
            o, i = out.ap, in_.ap
            P.emit(E, lambda e: e.tensor_copy(out=o, in_=i), [in_], [out])

        def recip(out, in_):
            o, i = out.ap, in_.ap
            P.emit(P.dve, lambda e: e.reciprocal(out=o, in_=i), [in_], [out])

        def memset(out, val, eng="dve"):
            E = P.dve if eng == "dve" else P.pool
            o = out.ap
            P.emit(E, lambda e: e.memset(o, val), [], [out])

        def pe(mms, final=True):
            reads, writes, seen = [], [], set()
            for (o, l, r, s0, s1) in mms:
                reads += [l, r]
                if id(o.root) not in seen or True:
                    writes.append(o)
            items = [(o.ap, l.ap, r.ap, s0, s1) for (o, l, r, s0, s1) in mms]

            def fn(e):
                ins = None
                for (o, l, r, s0, s1) in items:
                    ins = e.matmul(o, l, r, start=s0, stop=s1)
                return ins
            P.emit(P.pe, fn, reads, writes, final=True)

        rot = {"i": 0, "set": [0, 1, 2, 3, 4, 6, 7]}

        def nb():
            b = rot["set"][rot["i"] % len(rot["set"])]
            rot["i"] += 1
            return psb[b]

        SSB = psb[5]
        cnt = {"r": 0, "t": 0, "o": 0, "p": 0, "w": 0, "wd": 0, "y": 0, "c": 0, "s": 0, "d": 0, "ob": 0}

        def nxt(key, lst):
            v = lst[cnt[key] % len(lst)]
            cnt[key] += 1
            return v

        def spv(l, a, b=None):
            b = a + 1 if b is None else b
            return smallp[:, l * NSP + a:l * NSP + b]

        def rstd_from(ssv, D):
            sqv = nxt("s", sqrtT)
            act(sqv[:, :], ssv, AF.Ln, bias=epsT[:, 0:1], scale=1.0 / D)
            rv = nxt("r", rstd)
            act(rv[:, :], sqv[:, :], AF.Exp, scale=-0.5)
            return rv

        dma("sp", smallp.all(), D_in.v(smallp_d))
        dma("sp", corep.all(), D_in.v(corep_d))
        dma("sp", condS.all(), D_in.v(condT))
        dma("sp", shiftm.all(), D_in.v(shiftm_d))
        dma("sp", dft256.all(), D_in.v(dft256_d))
        dma("sp", dftc.all(), D_in.v(dftc_d))
        memset(ones.all(), 1.0)
        memset(epsT.all(), EPS)
        for i in range(2):
            memset(rEO[i].all(), 0.0)
        act(scond.all(), condS.all(), AF.Silu)
        def mod_pieces(l, pieces):
            for piece in pieces:
                wb = nxt("w", wbuf)
                dma("pool", wb.all(), D_in.v(w_ada[l, :, piece * 512:(piece + 1) * 512].rearrange("(k p) c -> p k c", p=128)))
                b = nb()
                mms = []
                for mi in range(4):
                    for k in range(8):
                        mms.append((b[:, mi * 2:mi * 2 + 2], wb[:, k, mi * 128:(mi + 1) * 128], scond[:, k, 0:2], k == 0, k == 7))
                pe(mms)
                b3 = b.ap[:, 0:8].rearrange("p (a c) -> p a c", c=2)
                for c in range(2):
                    tt(modA[:, l, piece * 4:(piece + 1) * 4, c], View(b3[:, :, c], b.root, 0, 2048),
                       spv(l, 32 + piece * 4, 36 + piece * 4), ALU.add)
            if pieces and pieces[-1] == 11:
                for c in range(2):
                    stt(gs1[:, l, :, c], modA[:, l, 8:16, c], 1.0, spv(l, 0, 8), ALU.add, ALU.mult)
                    tt(gg1[:, l, :, c], modA[:, l, 16:24, c], spv(l, 8, 16), ALU.mult)
                    stt(gs2[:, l, :, c], modA[:, l, 32:40, c], 1.0, spv(l, 16, 24), ALU.add, ALU.mult)
                    tt(gg2[:, l, :, c], modA[:, l, 40:48, c], spv(l, 24, 32), ALU.mult)

        if "mod" in stages:
            mod_pieces(0, list(range(12)))

        def load_layer_small(l):
            dma("pool", spatT.all(), D_in.v(spatT_d[l]))
            dma("sp", spb.all(), D_in.v(spb_d[l]))
            dma("pool", wuq.all(), D_in.v(w_uq_d[l]))
            dma("pool", wukv.all(), D_in.v(w_ukv_d[l]))

        def big_norm(xb, gsT, shiftcol, l, c):
            for m in range(8):
                act(sq[:, m, :], xb[:, m, :], AF.Square)
            pe([(SSB[:, :], ones[:, :], sq[:, m, :], m == 0, m == 7) for m in range(8)])
            rv = rstd_from(SSB[:, :], 1024.0)
            for m in range(8):
                t = nxt("t", tmpf)
                stt(t[:, :], xb[:, m, :], gsT[:, l, m, c:c + 1], rv[:, :], ALU.mult, ALU.mult)
                act(H[:, m, :], t[:, :], AF.Identity, bias=modA[:, l, shiftcol + m, c:c + 1])

        def post_norm_residual(osb, xb, ggT, l, c):
            pe([(SSB[:, :], ones[:, :], sq[:, m, :], m == 0, m == 7) for m in range(8)])
            rv = rstd_from(SSB[:, :], 1024.0)
            for m in range(8):
                t = nxt("t", tmpf)
                stt(t[:, :], osb[:, m, :], ggT[:, l, m, c:c + 1], rv[:, :], ALU.mult, ALU.mult)
                tt(xb[:, m, :], xb[:, m, :], t[:, :], ALU.add)

        def in_proj(l, sample, t0, dst):
            def wunit(c0, n):
                wb = nxt("w", wbuf)
                dma("pool", wb[:, :, 0:n], D_in.v(w_in_p[l, :, c0:c0 + n].rearrange("(k p) c -> p k c", p=128)))
                return wb

            def fgroup(wb, ci, M=128):
                b = nb()
                pe([(b[0:M, :], wb[:, k, ci * 128:ci * 128 + M], H[:, k, :], k == 0, k == 7) for k in range(8)])
                return b
            wb = wunit(0, 512)
            for ci in range(2):
                b = fgroup(wb, ci)
                act(ublk[:, ci, :], b[:, :], AF.Copy)
            for ci in range(2):
                b = fgroup(wb, 2 + ci)
                act(hcp[ci][:, :], b[:, :], AF.Copy)
            if not sample and plevel < 1.1:
                return
            wb = wunit(512, 512)
            for ci in range(2):
                b = fgroup(wb, ci)
                if sample:
                    tt(zext[:, ci, 0:512], hcp[ci][:, :], b[:, :], ALU.mult)
                else:
                    zd = View(zext.ap[:, ci, 0:516].rearrange("p (s t) -> p s t", s=2)[:, :, 1:257], zext.root,
                              zext[:, ci, 0:516].lo, zext[:, ci, 0:516].hi)
                    hv = View(hcp[ci].ap.rearrange("p (s t) -> p s t", s=2), hcp[ci].root, 0, 2048)
                    bv = View(b.ap.rearrange("p (s t) -> p s t", s=2), b.root, 0, 2048)
                    tt(zd, hv, bv, ALU.mult)
            for ci in range(2):
                b = fgroup(wb, 2 + ci)
                act(gbblk[:, ci, :], b[:, :], AF.Copy)
            if not sample and plevel < 1.2:
                return
            wb = wunit(1024, 512)
            bq0 = fgroup(wb, 0)
            bq1 = fgroup(wb, 1)
            bkv = fgroup(wb, 2)
            bkr = fgroup(wb, 3)
            act(sqq[:, 0, :], bq0[:, :], AF.Square)
            act(sqq[:, 1, :], bq1[:, :], AF.Square)
            act(sqq[:, 2, :], bkv[:, :], AF.Square)
            pe([(SSB[:, :], ones[:, :], sqq[:, 0, :], True, False), (SSB[:, :], ones[:, :], sqq[:, 1, :], False, True)])
            rv = rstd_from(SSB[:, :], 192.0)
            stt(cqn[:, 0, :], bq0[:, :], spv(l, 88), rv[:, :], ALU.mult, ALU.mult)
            stt(cqn[:, 1, :], bq1[:, :], spv(l, 89), rv[:, :], ALU.mult, ALU.mult)
            pe([(SSB[:, :], ones[:, :], sqq[:, 2, :], True, True)])
            rv2 = rstd_from(SSB[:, :], 128.0)
            if sample:
                stt(ckvb[:, :], bkv[:, :], spv(l, 90), rv2[:, :], ALU.mult, ALU.mult)
                dma("sp", D_xk_in.v(xk_in.ap()[0:128, t0:t0 + 512]), ckvb[:, :])
                wb3 = wunit(1536, 128)
                bkp = fgroup(wb3, 0)
                t1, t2 = nxt("t", tmpf), nxt("t", tmpf)
                tt(t1[64:96, :], bkr[64:96, :], ropeT[64:96, 0, t0:t0 + 512], ALU.mult)
                tt(t2[64:96, :], bkp[64:96, :], ropeT[64:96, 1, t0:t0 + 512], ALU.mult)
                tt(krst[64:96, :], t1[64:96, :], t2[64:96, :], ALU.add)
                dma("sp", D_xk_in.v(xk_in.ap()[128:160, t0:t0 + 512]), krst[64:96, :])
            else:
                stt(ckvf[:, :], bkv[:, :], spv(l, 90), rv2[:, :], ALU.mult, ALU.mult)
                dma("sp", D_out.fresh(ckv_o[l, :, t0:t0 + 512]), ckvf[:, :])
                cp(ckv_all[:, 0:512], ckvf[:, :])
                act(krf[64:96, :], bkr[64:96, :], AF.Copy)
                dma("sp", D_out.fresh(kr_o[l, :, t0:t0 + 512]), krf[64:96, :])
                cp(krst[64:96, :], krf[64:96, :])
            if not sample and plevel < 1.3:
                return
            for h in range(4):
                bq = nb()
                pe([(bq[:, :], wuq[:, 0, 0, h * 128:(h + 1) * 128], cqn[:, 0, :], True, False),
                    (bq[:, :], wuq[:, 0, 1, h * 128:(h + 1) * 128], cqn[:, 1, :], False, True)])
                if sample:
                    ts(Qblk[:, h, :], bq[:, :], 1.0, ALU.mult)
                else:
                    act(Qblk[:, h, :], bq[:, :], AF.Copy)
                if sample:
                    bp = nb()
                    pe([(bp[:, :], wuq[:, 1, 0, h * 128:(h + 1) * 128], cqn[:, 0, :], True, False),
                        (bp[:, :], wuq[:, 1, 1, h * 128:(h + 1) * 128], cqn[:, 1, :], False, True)])
                    t1, t2 = nxt("t", tmpf), nxt("t", tmpf)
                    tt(t1[64:96, :], bq[64:96, :], ropeT[64:96, 0, t0:t0 + 512], ALU.mult)
                    tt(t2[64:96, :], bp[64:96, :], ropeT[64:96, 1, t0:t0 + 512], ALU.mult)
                    tt(Qblk[64:96, h, :], t1[64:96, :], t2[64:96, :], ALU.add)
            if not sample and plevel < 1.4:
                return
            wb = wunit(1664, 512)
            for ti in range(4):
                b = nb()
                pe([(b[:, :], H[:, k, ti * 128:(ti + 1) * 128], wb[:, k, :], k == 0, k == 7) for k in range(8)])
                act(vptok[:, ti, :], b[:, :], AF.Copy)

        def chunk_mlp():
            for ci in range(4):
                b = nb()
                mms = []
                for pr in range(2):
                    for gi in range(2):
                        q0 = (pr * 2 + gi) * 128
                        mms.append((b[:, q0:q0 + 128], vptok[:, ci, pr * 128:(pr + 1) * 128], spatT[:, pr * 2 + gi, :], True, True))
                pe(mms)
                b4 = b.ap.rearrange("p (a g q) -> p a g q", a=2, g=2)
                for gi in range(2):
                    r0, r1 = gi * 64, gi * 64 + 64
                    t = nxt("t", tmpf)
                    tv = View(t.ap[r0:r1, 0:256].rearrange("p (a q) -> p a q", a=2), t.root, 0, 1024)
                    tt(tv, View(b4[r0:r1, :, gi, :], b.root, 0, 2048), spb[r0:r1, :, :], ALU.add)
                    tt(mixblk[r0:r1, 0:2, ci * 128:(ci + 1) * 128], tv, ublk[r0:r1, 0:2, ci * 128:(ci + 1) * 128], ALU.mult)

        def conv_seg(l, zsrc, ch, gbv, outv, L):
            y = nxt("c", convy)
            a = zsrc
            act(y[:, 0:L], zext[:, ch, a + 1:a + 1 + L], AF.Identity, bias=spv(l, 86 + ch), scale=spv(l, 80 + 2 + ch))
            stt(y[:, 0:L], zext[:, ch, a:a + L], spv(l, 80 + ch), y[:, 0:L], ALU.mult, ALU.add)
            stt(y[:, 0:L], zext[:, ch, a + 2:a + 2 + L], spv(l, 80 + 4 + ch), y[:, 0:L], ALU.mult, ALU.add)
            tt(outv, gbv, y[:, 0:L], ALU.mult)

        def attn_norm(ob, hh, outv_fn):
            osb_ = nxt("o", Osb)
            act(osb_[:, :], ob[:, :], AF.Copy)
            r = rEO[hh]
            d0, d1 = (64, 128) if hh == 0 else (0, 64)
            o0, o1 = (0, 64) if hh == 0 else (64, 128)
            recip(r[d0:d1, :], osb_[d0:d1, :])
            pe([(SSB[:, :], shiftm[:, :], r[:, :], True, True)])
            tt(outv_fn(o0, o1), osb_[o0:o1, :], SSB[o0:o1, :], ALU.mult)

        def ones_V2():
            for i in range(2):
                memset(V2[i].all(), 1.0)

        def build_V(h, hh, nkt):
            vs = 0 if hh == 0 else 64
            for k0 in range(0, nkt, 8):
                n = min(8, nkt - k0)
                b = nb()
                pe([(b[:, j * 64:(j + 1) * 64], ckv_all[:, (k0 + j) * 128:(k0 + j + 1) * 128],
                     wukv[:, 512 + h * 64:512 + (h + 1) * 64], True, True) for j in range(n)])
                bv = View(b.ap[:, 0:n * 64].rearrange("p (j v) -> p j v", v=64), b.root, 0, 2048)
                act(V2[hh][:, k0:k0 + n, vs:vs + 64], bv, AF.Copy)

        def build_Knope(h, hh, ncols):
            for c0 in range(0, ncols, 512):
                n = min(512, ncols - c0)
                b = nb()
                pe([(b[:, 0:n], wukv[:, h * 128:(h + 1) * 128], ckv_all[:, c0:c0 + n], True, True)])
                ts(K2[hh][:, c0:c0 + n], b[:, 0:n], 1.0, ALU.mult)

        def ffn_and_rest(l, c, xb, dst_store):
            big_norm(xb, gs2, 24, l, c)
            for unit in range(11):
                wb = nxt("w", wbuf)
                dma("pool", wb.all(), D_in.v(w_gu_p[l, :, unit * 512:(unit + 1) * 512].rearrange("(k p) c -> p k c", p=128)))
                for jj in range(2):
                    j = unit * 2 + jj
                    bg, bu = nb(), nb()
                    pe([(bg[:, :], wb[:, k, jj * 256:jj * 256 + 128], H[:, k, :], k == 0, k == 7) for k in range(8)])
                    pe([(bu[:, :], wb[:, k, jj * 256 + 128:jj * 256 + 256], H[:, k, :], k == 0, k == 7) for k in range(8)])
                    s_ = nxt("d", sg)
                    act(s_[:, :], bg[:, :], AF.Silu)
                    tt(Affn[:, j, :], s_[:, :], bu[:, :], ALU.mult)
            osb = f32big[1]
            for du in range(4):
                wd = nxt("wd", wdbuf)
                dma("pool", wd.all(), D_in.v(w_down[l, :, du * 256:(du + 1) * 256].rearrange("(j p) c -> p j c", p=128)))
                for mi in range(2):
                    m = du * 2 + mi
                    b = nb()
                    pe([(b[:, :], wd[:, j, mi * 128:(mi + 1) * 128], Affn[:, j, :], j == 0, j == 21) for j in range(22)])
                    act(osb[:, m, :], b[:, :], AF.Copy)
                    act(sq[:, m, :], b[:, :], AF.Square)
            post_norm_residual(osb, xb, gg2, l, c)
            for m in range(8):
                dma("sp", View(dst_store.ap[:, m, :], dst_store.root, 0, 1), xb[:, m, :])

        def out_proj_residual(l, c, xb):
            osb = f32big[1]
            for wu in range(2):
                wb = nxt("w", wbuf)
                dma("pool", wb.all(), D_in.v(w_out[l, :, wu * 512:(wu + 1) * 512].rearrange("(k p) c -> p k c", p=128)))
                for mi in range(4):
                    m = wu * 4 + mi
                    b = nb()
                    pe([(b[:, :], wb[:, k, mi * 128:(mi + 1) * 128], mixblk[:, k, :], k == 0, k == 7) for k in range(8)])
                    act(osb[:, m, :], b[:, :], AF.Copy)
                    act(sq[:, m, :], b[:, :], AF.Square)
            post_norm_residual(osb, xb, gg1, l, c)

        def load_x(xb, src):
            for m in range(8):
                dma("sp", xb[:, m, :], View(src.ap[:, m, :], src.root, 0, 1))

        def xsrc(l, sample, t0):
            if l == 0:
                base = xT_s if sample else xT_p
                return D_in.v(base[:, :, t0:t0 + 512].rearrange("c p t -> p c t"))
            D = D_xs_s if sample else D_xs_p
            return D.v(D.ap[:, :, t0:t0 + 512].rearrange("c p t -> p c t"))

        def xdst(l, sample, t0):
            if l == depth - 1:
                base = ys_o if sample else yp_o
                return D_out.fresh(base[:, :, t0:t0 + 512].rearrange("c p t -> p c t"))
            D = D_xs_s if sample else D_xs_p
            return D.v(D.ap[:, :, t0:t0 + 512].rearrange("c p t -> p c t"))

        def prompt_block(l, blk):
            t0 = blk * 512
            xb = f32big[0]
            load_x(xb, xsrc(l, False, t0))
            big_norm(xb, gs1, 0, l, 0)
            if plevel < 0.5:
                return
            in_proj(l, False, t0, None)
            if plevel < 2:
                return
            chunk_mlp()
            for s in range(2):
                for ch in range(2):
                    conv_seg(l, s * 258, ch, gbblk[:, ch, s * 256:(s + 1) * 256], mixblk[:, 2 + ch, s * 256:(s + 1) * 256], 256)
            if plevel < 3:
                return
            for s in range(2):
                bu, bv = nb(), nb()
                for tb, bb in ((0, bu), (1, bv)):
                    mms = []
                    for pr in range(2):
                        for ti in range(2):
                            mms.append((bb[:, pr * 256:(pr + 1) * 256], vptok[:, s * 2 + ti, 256 + pr * 128:256 + (pr + 1) * 128],
                                        dft256[:, tb, ti, :], ti == 0, ti == 1))
                    pe(mms)
                act(UVsb[:, 0, :], bu[:, :], AF.Copy)
                ts(UVsb[:, 1, :], bv[:, :], 1.0, ALU.mult)
                by = nb()
                mms = []
                for pr in range(2):
                    mms.append((by[:, pr * 256:(pr + 1) * 256], dftc[:, 0, :], UVsb[:, 0, pr * 256:(pr + 1) * 256], True, False))
                    mms.append((by[:, pr * 256:(pr + 1) * 256], dftc[:, 1, :], UVsb[:, 1, pr * 256:(pr + 1) * 256], False, True))
                pe(mms)
                byv = View(by.ap.rearrange("p (a m) -> p a m", a=2), by.root, 0, 2048)
                act(mixblk[:, 4:6, s * 256:(s + 1) * 256], byv, AF.Copy)
            if plevel < 4:
                return
            rot["set"] = [0, 1, 2, 3, 4]
            ones_V2()
            for h in range(4):
                hh = h % 2
                build_Knope(h, hh, 512)
                cp(K2[hh][64:96, 0:512], krst[64:96, :])
                build_V(h, hh, 4)
                ob = psb[6 + (cnt["ob"] % 2)]
                cnt["ob"] += 1
                for s in range(2):
                    for k2 in range(2):
                        kt = s * 2 + k2
                        bs = nb()
                        pe([(bs[:, 0:256], K2[hh][:, kt * 128:(kt + 1) * 128], Qblk[:, h, s * 256:(s + 1) * 256], True, True)])
                        pt = nxt("p", Pt)
                        act(pt[:, 0:256], bs[:, 0:256], AF.Exp, scale=ATT_SCALE)
                        pe([(ob[:, s * 256:(s + 1) * 256], V2[hh][:, kt, :], pt[:, 0:256], k2 == 0, k2 == 1)],
                           final=(s == 1 and k2 == 1))
                attn_norm(ob, hh, lambda o0, o1, h=h: mixblk[o0:o1, 6 + h // 2, :])
            rot["set"] = [0, 1, 2, 3, 4, 6, 7]
            if plevel < 5:
                return
            out_proj_residual(l, 0, xb)
            if plevel < 6:
                return
            ffn_and_rest(l, 0, xb, xdst(l, False, t0))

        def sample_A_block(l, blk):
            t0 = blk * 512
            if "mod" in stages and l + 1 < depth:
                mod_pieces(l + 1, list(range(blk * 3, blk * 3 + 3)))
            xb = f32big[0]
            load_x(xb, xsrc(l, True, t0))
            big_norm(xb, gs1, 0, l, 1)
            in_proj(l, True, t0, None)
            chunk_mlp()
            dma("sp", D_mixS.v(mixS.ap()[0:2, :, t0:t0 + 512].rearrange("c p t -> p c t")), mixblk[:, 0:2, :])
            dma("sp", D_zS.v(zS.ap()[:, :, 1 + t0:1 + t0 + 512].rearrange("c p t -> p c t")), zext[:, :, 0:512])
            dma("sp", D_gbS.v(gbS.ap()[:, :, t0:t0 + 512].rearrange("c p t -> p c t")), gbblk[:, :, :])
            dma("sp", D_QS.v(QS.ap()[:, :, t0:t0 + 512].rearrange("h p t -> p h t")), Qblk[:, :, :])
            dma("sp", D_pc_in.v(pc_in.ap()[t0:t0 + 512, :].rearrange("(a p) c -> p a c", p=128)), vptok[:, :, 256:512])
            if blk == 0:
                for ch in range(2):
                    dma("sp", D_xk_in.v(xk_in.ap()[160 + ch, :].rearrange("(p c) -> p c", c=16)), zext[:, ch, 0:16])
            if blk == 3:
                for ch in range(2):
                    dma("sp", D_xk_in.v(xk_in.ap()[162 + ch, :].rearrange("(p c) -> p c", c=16)), zext[:, ch, 496:512])

        def exchange():
            a_in, a_out = xk_in.ap().opt(), xk_out.ap().opt()
            P.emit(P.pool, lambda e: e.collective_compute("AllGather", ALU.bypass, replica_groups=groups,
                                                          ins=[a_in], outs=[a_out]),
                   [D_xk_in.v()], [D_xk_out.v()], kind="cc")
            b_in, b_out = pc_in.ap().opt(), pc_out.ap().opt()
            P.emit(P.pool, lambda e: e.collective_compute("AllGather", ALU.bypass, replica_groups=groups,
                                                          ins=[b_in], outs=[b_out]),
                   [D_pc_in.v()], [D_pc_out.v()], kind="cc")

        def sample_mix(l):
            xo = xk_out.ap()
            for ch in range(2):
                dma("sp", hal[:, ch, 0, :], D_xk_out.v(xo[162 + ch, :].rearrange("(p c) -> p c", c=16)))
                dma("sp", hal[:, ch, 1, :], D_xk_out.v(xo[164 + 160 + ch, :].rearrange("(p c) -> p c", c=16)))
            ts(halm[:, :, 0], hal[:, :, 0, 15], corep[:, 0:1], ALU.mult)
            ts(halm[:, :, 1], hal[:, :, 1, 0], corep[:, 1:2], ALU.mult)
            def conv_block(blk):
                t0 = blk * 512
                if blk == 0:
                    dma("sp", zext[:, :, 1:514], D_zS.v(zS.ap()[:, :, 1:514].rearrange("c p t -> p c t")))
                    for ch in range(2):
                        cp(zext[:, ch, 0:1], halm[:, ch, 0:1])
                elif blk == 3:
                    dma("sp", zext[:, :, 0:513], D_zS.v(zS.ap()[:, :, t0:t0 + 513].rearrange("c p t -> p c t")))
                    for ch in range(2):
                        cp(zext[:, ch, 513:514], halm[:, ch, 1:2])
                else:
                    dma("sp", zext[:, :, 0:514], D_zS.v(zS.ap()[:, :, t0:t0 + 514].rearrange("c p t -> p c t")))
                dma("sp", gbblk[:, :, :], D_gbS.v(gbS.ap()[:, :, t0:t0 + 512].rearrange("c p t -> p c t")))
                ym = nxt("y", ymix)
                for ch in range(2):
                    conv_seg(l, 0, ch, gbblk[:, ch, :], ym[:, ch, :], 512)
                dma("sp", D_mixS.v(mixS.ap()[2:4, :, t0:t0 + 512].rearrange("c p t -> p c t")), ym[:, :, :])
            dma("sp", pc_all.all(), D_pc_out.v(pc_out.ap().rearrange("(a p) c -> p a c", p=128)))
            rot["set"] = [0, 1, 2, 3]
            acc = [psb[4], psb[5], psb[6], psb[7]]
            for mb in range(4):
                for ng in range(4):
                    ct, st_ = nxt("d", dftbuf), nxt("d", dftbuf)
                    for tb, buf in ((0, ct), (1, st_)):
                        dma("sp", buf.all(), D_in.v(dft4096_d[tb, ng * 1024:(ng + 1) * 1024, mb * 512:(mb + 1) * 512]
                                                     .rearrange("(a p) m -> p a m", p=128)))
                    mms = []
                    for a in range(8):
                        nt = ng * 8 + a
                        for pr in range(2):
                            mms.append((acc[pr][:, :], pc_all[:, nt, pr * 128:(pr + 1) * 128], ct[:, a, :], nt == 0, nt == 31))
                            mms.append((acc[2 + pr][:, :], pc_all[:, nt, pr * 128:(pr + 1) * 128], st_[:, a, :], nt == 0, nt == 31))
                    pe(mms, final=(ng == 3))
                for i in range(4):
                    if i % 2 == 0:
                        act(UVsb[:, i, :], acc[i][:, :], AF.Copy)
                    else:
                        ts(UVsb[:, i, :], acc[i][:, :], 1.0, ALU.mult)
                ym = nxt("y", ymix)
                for pr in range(2):
                    by = nb()
                    pe([(by[:, :], dftc[:, 0, :], UVsb[:, pr, :], True, False), (by[:, :], dftc[:, 1, :], UVsb[:, 2 + pr, :], False, True)])
                    act(ym[:, pr, :], by[:, :], AF.Copy)
                dma("sp", D_mixS.v(mixS.ap()[4:6, :, mb * 512:(mb + 1) * 512].rearrange("c p t -> p c t")), ym[:, :, :])
                conv_block(mb)
            rot["set"] = [0, 1, 2, 3, 4]
            ones_V2()
            dma("sp", ckv_all[:, 0:2048], D_xk_out.v(xo[0:128, :]))
            dma("sp", ckv_all[:, 2048:4096], D_xk_out.v(xo[164:292, :]))
            dma("pool", ckv_all[:, 4096:4352], D_in.v(cacheT_ckv[l]))
            for h in range(4):
                hh = h % 2
                build_Knope(h, hh, 4352)
                dma("sp", K2[hh][64:96, 0:2048], D_xk_out.v(xo[128:160, :]))
                dma("sp", K2[hh][64:96, 2048:4096], D_xk_out.v(xo[292:324, :]))
                dma("pool", K2[hh][64:96, 4096:4352], D_in.v(cacheT_kr[l]))
                build_V(h, hh, 34)
                qh = nxt("c", Qh)
                dma("sp", qh[:, :], D_QS.v(QS.ap()[h, :, :]))
                for qb in range(4):
                    ob = psb[6 + (cnt["ob"] % 2)]
                    cnt["ob"] += 1
                    pts = {}

                    def s_step(kt):
                        bs = nb()
                        pe([(bs[:, :], K2[hh][:, kt * 128:(kt + 1) * 128], qh[:, qb * 512:(qb + 1) * 512], True, True)])
                        pt = nxt("p", Pt)
                        act(pt[:, :], bs[:, :], AF.Exp, scale=ATT_SCALE)
                        pts[kt] = pt
                    s_step(0)
                    s_step(1)
                    for kt in range(34):
                        if kt + 2 < 34:
                            s_step(kt + 2)
                        pe([(ob[:, :], V2[hh][:, kt, :], pts.pop(kt)[:, :], kt == 0, kt == 33)], final=(kt == 33))
                    ym = nxt("y", ymix)
                    attn_norm(ob, hh, lambda o0, o1: ym[o0:o1, 0, :])
                    o0, o1 = (0, 64) if hh == 0 else (64, 128)
                    dma("sp", D_mixS.v(mixS.ap()[6 + h // 2, o0:o1, qb * 512:(qb + 1) * 512]), ym[o0:o1, 0, :])
            rot["set"] = [0, 1, 2, 3, 4, 6, 7]

        def sample_F_block(l, blk):
            t0 = blk * 512
            xb = f32big[0]
            load_x(xb, xsrc(l, True, t0))
            dma("sp", mixblk.all(), D_mixS.v(mixS.ap()[:, :, t0:t0 + 512].rearrange("c p t -> p c t")))
            out_proj_residual(l, 1, xb)
            ffn_and_rest(l, 1, xb, xdst(l, True, t0))

        memset(zext.all(), 0.0)
        for l in range(depth):
            load_layer_small(l)
            dma("sp", ropeT.all(), D_in.v(ropeT_d.rearrange("a p t -> p a t")))
            if "SA" in stages:
                for blk in range(4):
                    sample_A_block(l, blk)
            memset(zext.all(), 0.0)
            if "P" in stages:
                prompt_block(l, 0)
            if "EX" in stages:
                exchange()
            if "P" in stages:
                prompt_block(l, 1)
            if "SM" in stages:
                sample_mix(l)
            if "SF" in stages:
                for blk in range(4):
                    sample_F_block(l, blk)

        fin = []
        for q in (P.sp, P.pool):
            for name, n in q.slots:
                if n > 0 and P.sp.waited.get(name, 0) < 16 * n:
                    fin.append((name, 16 * n))
        P.sp.ops.append((fin, None, None))

        semnames = ["pe", "act", "dve", "pool", "cc"] + [s_[0] for q in (P.sp, P.pool) for s_ in q.slots]
        sems = {n: st.enter_context(nc.semaphore(n)) for n in semnames}
        block = st.enter_context(nc.Block())

        def replay(eng):
            def run(e):
                for waits, fn, inc in eng.ops:
                    for k, v in waits:
                        e.wait_ge(sems[k], v)
                    if fn is None:
                        continue
                    ins = fn(e)
                    if inc is not None:
                        if inc[1] is None:
                            ins.then_inc(sems[inc[0]])
                        else:
                            ins.then_inc(sems[inc[0]], inc[1])
            return run

        block.tensor(replay(P.pe))
        block.scalar(replay(P.act))
        block.vector(replay(P.dve))
        block.gpsimd(replay(P.pool))
        block.sync(replay(P.sp))
    return nc


_CACHE = {}


def _consts():
    if "c" in _CACHE:
        return _CACHE["c"]
    bf = ml_dtypes.bfloat16
    n = np.arange(256)
    ang = 2 * np.pi * ((n[:, None] * n[None, :]) % 256) / 256.0
    C, S = np.cos(ang) / 16.0, np.sin(ang) / 16.0
    dft256 = np.stack([C, S], 0).reshape(2, 2, 128, 256).transpose(2, 0, 1, 3).astype(bf)
    ch = np.arange(64)
    angc = 2 * np.pi * ((ch[:, None] * ch[None, :]) % 64) / 64.0
    Cc, Sc = np.cos(angc) / 8.0, np.sin(angc) / 8.0
    CcB = np.zeros((128, 128)); nScB = np.zeros((128, 128))
    for g in range(2):
        CcB[g * 64:(g + 1) * 64, g * 64:(g + 1) * 64] = Cc
        nScB[g * 64:(g + 1) * 64, g * 64:(g + 1) * 64] = -Sc
    dftc = np.stack([CcB, nScB], 1).astype(bf)
    nn = np.arange(4096, dtype=np.int64)
    dft4096 = []
    for h in range(2):
        mm = h * 2048 + np.arange(2048, dtype=np.int64)
        k = (nn[:, None] * mm[None, :]) % 4096
        a = 2 * np.pi * k / 4096.0
        dft4096.append(np.stack([np.cos(a) / 64.0, np.sin(a) / 64.0], 0).astype(bf))
    shiftm = np.zeros((128, 128), np.float32)
    for m in range(128):
        shiftm[(m + 64) % 128, m] = 1.0
    rope = []
    inv = (10000.0 ** (-np.arange(0, 16, 2, dtype=np.float32) / 16.0)).astype(np.float32)
    for h in range(2):
        m = h * 2048 + np.arange(2048)
        pos = np.stack([(m // 64).astype(np.float32), (m % 64).astype(np.float32)], 0)
        tab = np.zeros((2, 128, 2048), np.float32)
        for r in range(32):
            a, b, j = r // 16, (r % 16) // 8, r % 8
            angr = (pos[a] * inv[j]).astype(np.float32)
            tab[0, 64 + r] = np.cos(angr)
            tab[1, 64 + r] = (-np.sin(angr)) if b == 0 else np.sin(angr)
        rope.append(tab)
    _CACHE["c"] = dict(dft256=dft256, dftc=dftc, dft4096=dft4096, shiftm=shiftm, rope=rope)
    return _CACHE["c"]


def kernel(x_prompt, x_sample, cache_ckv, cache_krope, c, c_ctx, w_ada, b_ada,
           g_pre_mix, g_post_mix, g_pre_ffn, g_post_ffn, w_in, spat_w, spat_b,
           conv_w, conv_b, g_q_lora, w_uq, g_kv_lora, w_ukv, w_out, w_gate_up, w_down):
    f = lambda a: np.ascontiguousarray(np.asarray(a, dtype=np.float32))
    x_prompt, x_sample, cache_ckv, cache_krope, c, c_ctx = map(f, (x_prompt, x_sample, cache_ckv, cache_krope, c, c_ctx))
    w_ada, b_ada, w_in, spat_w, spat_b, conv_w, conv_b = map(f, (w_ada, b_ada, w_in, spat_w, spat_b, conv_w, conv_b))
    g_pre_mix, g_post_mix, g_pre_ffn, g_post_ffn = map(f, (g_pre_mix, g_post_mix, g_pre_ffn, g_post_ffn))
    g_q_lora, w_uq, g_kv_lora, w_ukv, w_out, w_gate_up, w_down = map(f, (g_q_lora, w_uq, g_kv_lora, w_ukv, w_out, w_gate_up, w_down))
    K = _consts()
    L = DEPTH
    perm = np.array([(r // 16) * 16 + (1 - (r % 16) // 8) * 8 + r % 8 for r in range(32)])
    w_in_p = np.zeros((L, 1024, 2176), np.float32)
    w_in_p[:, :, 0:256] = w_in[:, :, 0:256]
    w_in_p[:, :, 256:512] = w_in[:, :, 512:768]
    w_in_p[:, :, 512:768] = w_in[:, :, 1024:1280]
    w_in_p[:, :, 768:1024] = w_in[:, :, 768:1024]
    w_in_p[:, :, 1024:1216] = w_in[:, :, 1536:1728]
    w_in_p[:, :, 1280:1408] = w_in[:, :, 1728:1856]
    w_in_p[:, :, 1408 + 64:1408 + 96] = w_in[:, :, 1856:1888]
    w_in_p[:, :, 1536 + 64:1536 + 96] = w_in[:, :, 1856 + perm]
    w_in_p[:, :, 1664:1920] = w_in[:, :, 256:512]
    w_in_p[:, :, 1920:2176] = w_in[:, :, 1280:1536]
    w_gu_p = np.ascontiguousarray(
        np.stack([w_gate_up[:, :, :2816].reshape(L, 1024, 22, 128), w_gate_up[:, :, 2816:].reshape(L, 1024, 22, 128)], 3)
        .reshape(L, 1024, 5632))
    spatT = np.ascontiguousarray(spat_w.transpose(0, 3, 1, 2))
    spb_bc = np.zeros((L, 128, 2, 128), np.float32)
    for pr in range(2):
        spb_bc[:, 0:64, pr, :] = spat_b[:, 2 * pr, None, :]
        spb_bc[:, 64:128, pr, :] = spat_b[:, 2 * pr + 1, None, :]
    w_uq_p = np.zeros((L, 128, 2, 2, 512), np.float32)
    for h in range(4):
        w_uq_p[:, :, 0, 0, h * 128:h * 128 + 96] = w_uq[:, 0:128, h * 96:(h + 1) * 96]
        w_uq_p[:, 0:64, 0, 1, h * 128:h * 128 + 96] = w_uq[:, 128:192, h * 96:(h + 1) * 96]
        src = h * 96 + 64 + perm
        w_uq_p[:, :, 1, 0, h * 128 + 64:h * 128 + 96] = w_uq[:, 0:128, src]
        w_uq_p[:, 0:64, 1, 1, h * 128 + 64:h * 128 + 96] = w_uq[:, 128:192, src]
    w_ukv_p = np.zeros((L, 128, 768), np.float32)
    _kv = w_ukv.reshape(L, 128, 4, 2, 64)
    for h in range(4):
        w_ukv_p[:, :, h * 128:h * 128 + 64] = _kv[:, :, h, 0, :]
        w_ukv_p[:, :, 512 + h * 64:512 + (h + 1) * 64] = _kv[:, :, h, 1, :]
    smallp = np.zeros((128, L * NSP), np.float32)
    v8 = lambda v: v.reshape(8, 128).T
    for l in range(L):
        o = l * NSP
        smallp[:, o + 0:o + 8] = v8(g_pre_mix[l])
        smallp[:, o + 8:o + 16] = v8(g_post_mix[l])
        smallp[:, o + 16:o + 24] = v8(g_pre_ffn[l])
        smallp[:, o + 24:o + 32] = v8(g_post_ffn[l])
        smallp[:, o + 32:o + 80] = b_ada[l].reshape(48, 128).T
        for k in range(3):
            smallp[:, o + 80 + 2 * k:o + 82 + 2 * k] = conv_w[l, k].reshape(2, 128).T
        smallp[:, o + 86:o + 88] = conv_b[l].reshape(2, 128).T
        smallp[:, o + 88] = g_q_lora[l, 0:128]
        smallp[0:64, o + 89] = g_q_lora[l, 128:192]
        smallp[:, o + 90] = g_kv_lora[l]
    shared = dict(w_ada=w_ada, w_in_p=w_in_p, w_out=w_out, w_gu_p=w_gu_p, w_down=w_down, spatT=spatT, spb_bc=spb_bc,
                  w_uq_p=w_uq_p, w_ukv_p=w_ukv_p, smallp=smallp, dft256=K["dft256"], dftc=K["dftc"], shiftm=K["shiftm"])
    in_maps = []
    for i in range(8):
        b, h = i // 2, i % 2
        xp = x_prompt[4 * i:4 * i + 4].reshape(TP, 1024)
        xs = x_sample[b, h * TS:(h + 1) * TS]
        cond = np.stack([c_ctx, c[b]], -1)
        corep = np.zeros((128, 2), np.float32)
        corep[:, 0] = float(h)
        corep[:, 1] = float(1 - h)
        m = dict(shared)
        m.update(
            xT_p=np.ascontiguousarray(xp.T.reshape(8, 128, TP)),
            xT_s=np.ascontiguousarray(xs.T.reshape(8, 128, TS)),
            condT=np.ascontiguousarray(cond.reshape(8, 128, 2).transpose(1, 0, 2)),
            cacheT_ckv=np.ascontiguousarray(cache_ckv[b].transpose(0, 2, 1)),
            cacheT_kr=np.ascontiguousarray(cache_krope[b].transpose(0, 2, 1)),
            corep=corep, ropeT=K["rope"][h], dft4096=K["dft4096"][h],
        )
        in_maps.append(m)
    if _CACHE.get("prep_only"):
        return in_maps
    if "nc" not in _CACHE:
        _CACHE["nc"] = build_program()
    res = run_bass_kernel_spmd(_CACHE["nc"], in_maps, core_ids=list(range(8)))
    return _post(res.results)


def _post(results):
    class _R:
        pass
    res = _R()
    res.results = results
    y_prompt = np.zeros((32, 256, 1024), np.float32)
    y_sample = np.zeros((4, 4096, 1024), np.float32)
    state_ckv = np.zeros((32, DEPTH, 256, 128), np.float32)
    state_kr = np.zeros((32, DEPTH, 256, 32), np.float32)
    for i in range(8):
        r = res.results[i]
        b, h = i // 2, i % 2
        y_prompt[4 * i:4 * i + 4] = np.asarray(r["yp_o"]).reshape(1024, TP).T.reshape(4, 256, 1024)
        y_sample[b, h * TS:(h + 1) * TS] = np.asarray(r["ys_o"]).reshape(1024, TS).T
        state_ckv[4 * i:4 * i + 4] = np.asarray(r["ckv_o"]).transpose(2, 0, 1).reshape(4, 256, DEPTH, 128).transpose(0, 2, 1, 3)
        state_kr[4 * i:4 * i + 4] = np.asarray(r["kr_o"]).transpose(2, 0, 1).reshape(4, 256, DEPTH, 32).transpose(0, 2, 1, 3)
    return (y_prompt, y_sample, state_ckv, state_kr)
```

```python
import numpy as np
import ml_dtypes
from contextlib import ExitStack
import concourse.bass as bass
import concourse.mybir as mybir
from concourse.bass_utils import run_bass_kernel_spmd

F32 = mybir.dt.float32
BF16 = mybir.dt.bfloat16
AF = mybir.ActivationFunctionType
ALU = mybir.AluOpType

DEPTH = 4
NSP = 91
TP, TS = 1024, 2048
EPS = 1e-6
ATT_SCALE = float((64 + 32) ** -0.5)
NSLOT = 14


class Cell:
    __slots__ = ("w", "r")

    def __init__(s):
        s.w = None
        s.r = {}


class Root:
    def __init__(s, cs):
        s.cs = cs
        s.cells = {}

    def get(s, lo, hi):
        out = []
        for i in range(lo // s.cs, (hi - 1) // s.cs + 1):
            c = s.cells.get(i)
            if c is None:
                c = s.cells[i] = Cell()
            out.append(c)
        return out


class View:
    __slots__ = ("ap", "root", "lo", "hi")

    def __init__(s, ap, root, lo, hi):
        s.ap, s.root, s.lo, s.hi = ap, root, lo, hi

    def cells(s):
        return s.root.get(s.lo, s.hi)


class Tile:
    def __init__(s, ap, shape, es, root=None, boff=0, cs=512):
        s.ap, s.shape, s.es = ap, list(shape), es
        s.root = root if root is not None else Root(cs)
        s.boff = boff
        st = [1] * len(shape)
        for i in range(len(shape) - 2, 0, -1):
            st[i] = st[i + 1] * shape[i + 1]
        s.st = st

    def __getitem__(s, idx):
        if not isinstance(idx, tuple):
            idx = (idx,)
        idx = tuple(idx) + (slice(None),) * (len(s.shape) - len(idx))
        lo = 0
        hi = 0
        for d in range(1, len(s.shape)):
            ix = idx[d]
            if isinstance(ix, int):
                a, b = ix, ix + 1
            else:
                a = 0 if ix.start is None else ix.start
                b = s.shape[d] if ix.stop is None else ix.stop
            assert 0 <= a < b <= s.shape[d], (idx, s.shape)
            lo += a * s.st[d]
            hi += (b - 1) * s.st[d]
        return View(s.ap[idx], s.root, s.boff + lo * s.es, s.boff + (hi + 1) * s.es)

    def all(s):
        return s[tuple(slice(None) for _ in s.shape)]


class DTile:
    def __init__(s, ap):
        s.ap = ap
        s.root = Root(1 << 40)

    def v(s, ap=None):
        return View(s.ap if ap is None else ap, s.root, 0, 1)

    def fresh(s, ap):
        return View(ap, Root(1 << 40), 0, 1)


class Eng:
    def __init__(s, name, is_pe=False):
        s.name = name
        s.is_pe = is_pe
        s.ops = []
        s.count = 0
        s.waited = {}
        s.slots = []
        s.slot_i = 0


class Prog:
    def __init__(s):
        s.pe = Eng("pe", True)
        s.act = Eng("act")
        s.dve = Eng("dve")
        s.pool = Eng("pool")
        s.sp = Eng("sp")
        s.cc_count = 0
        for q in (s.sp, s.pool):
            q.slots = [[f"{q.name}_d{i}", 0] for i in range(NSLOT)]

    def emit(s, eng, fn, reads, writes, kind="op", final=True):
        deps = {}

        def add(sig):
            if sig is not None and deps.get(sig[0], 0) < sig[1]:
                deps[sig[0]] = sig[1]

        rc = [c for v in reads for c in v.cells()]
        wc = [c for v in writes for c in v.cells()]
        for c in rc:
            add(c.w)
        for c in wc:
            add(c.w)
            for k, val in c.r.items():
                add((k, val))
        waits = []
        for k, v in deps.items():
            if eng.is_pe and k == eng.name:
                continue
            if eng.waited.get(k, 0) >= v:
                continue
            eng.waited[k] = v
            waits.append((k, v))
        if kind == "dma":
            slot = eng.slots[eng.slot_i]
            eng.slot_i = (eng.slot_i + 1) % len(eng.slots)
            prev = 16 * slot[1]
            if prev > 0 and eng.waited.get(slot[0], 0) < prev:
                eng.waited[slot[0]] = prev
                waits.append((slot[0], prev))
            slot[1] += 1
            sig = (slot[0], 16 * slot[1])
            inc = (slot[0], 16)
        elif kind == "cc":
            s.cc_count += 1
            sig = ("cc", s.cc_count)
            inc = ("cc", None)
        else:
            sig = (eng.name, eng.count + 1)
            if final:
                eng.count += 1
                inc = (eng.name, 1)
            else:
                inc = None
        eng.ops.append((waits, fn, inc))
        for c in rc:
            if c.r.get(sig[0], 0) < sig[1]:
                c.r[sig[0]] = sig[1]
        for c in wc:
            c.w = sig
            c.r = {}
        return sig


def build_program(depth=DEPTH, groups=None, stages="mod,SA,EX,P,SM,SF"):
    groups = groups or [[0, 1], [2, 3], [4, 5], [6, 7]]
    stages = set(stages.split(","))
    plevel = 9
    for t_ in list(stages):
        if t_.startswith("P:"):
            plevel = float(t_[2:])
            stages.add("P")
    nc = bass.Bass("TRN2", target_bir_lowering=False)
    P = Prog()

    def din(name, shape, dt=F32):
        return nc.dram_tensor(name, list(shape), dt, kind="ExternalInput").ap()

    def dout(name, shape, dt=F32):
        return nc.dram_tensor(name, list(shape), dt, kind="ExternalOutput").ap()

    def dscr(name, shape, dt=BF16):
        return nc.dram_tensor(name, list(shape), dt)

    xT_p = din("xT_p", [8, 128, TP])
    xT_s = din("xT_s", [8, 128, TS])
    condT = din("condT", [128, 8, 2])
    cacheT_ckv = din("cacheT_ckv", [DEPTH, 128, 256])
    cacheT_kr = din("cacheT_kr", [DEPTH, 32, 256])
    w_ada = din("w_ada", [DEPTH, 1024, 6144])
    w_in_p = din("w_in_p", [DEPTH, 1024, 2176])
    w_out = din("w_out", [DEPTH, 1024, 1024])
    w_gu_p = din("w_gu_p", [DEPTH, 1024, 5632])
    w_down = din("w_down", [DEPTH, 2816, 1024])
    spatT_d = din("spatT", [DEPTH, 128, 4, 128])
    spb_d = din("spb_bc", [DEPTH, 128, 2, 128])
    w_uq_d = din("w_uq_p", [DEPTH, 128, 2, 2, 512])
    w_ukv_d = din("w_ukv_p", [DEPTH, 128, 768])
    smallp_d = din("smallp", [128, DEPTH * NSP])
    corep_d = din("corep", [128, 2])
    ropeT_d = din("ropeT", [2, 128, TS])
    dft256_d = din("dft256", [128, 2, 2, 256], BF16)
    dft4096_d = din("dft4096", [2, 4096, 2048], BF16)
    dftc_d = din("dftc", [128, 2, 128], BF16)
    shiftm_d = din("shiftm", [128, 128])

    yp_o = dout("yp_o", [8, 128, TP])
    ys_o = dout("ys_o", [8, 128, TS])
    ckv_o = dout("ckv_o", [DEPTH, 128, TP])
    kr_o = dout("kr_o", [DEPTH, 32, TP])

    xs_p = dscr("xs_p", [8, 128, TP], F32)
    xs_s = dscr("xs_s", [8, 128, TS], F32)
    zS = dscr("zS", [2, 128, TS + 2])
    gbS = dscr("gbS", [2, 128, TS])
    QS = dscr("QS", [4, 128, TS])
    mixS = dscr("mixS", [8, 128, TS])
    xk_in = dscr("xk_in", [164, TS])
    xk_out = dscr("xk_out", [328, TS])
    pc_in = dscr("pc_in", [TS, 256])
    pc_out = dscr("pc_out", [2 * TS, 256])
    D_xs_p, D_xs_s = DTile(xs_p.ap()), DTile(xs_s.ap())
    D_zS, D_gbS, D_QS, D_mixS = DTile(zS.ap()), DTile(gbS.ap()), DTile(QS.ap()), DTile(mixS.ap())
    D_xk_in, D_xk_out = DTile(xk_in.ap()), DTile(xk_out.ap())
    D_pc_in, D_pc_out = DTile(pc_in.ap()), DTile(pc_out.ap())
    D_in = DTile(xT_p)
    D_out = DTile(yp_o)

    es_of = {F32: 4, BF16: 2}

    with ExitStack() as st:
        def sb(name, shape, dt, cs=512):
            h = st.enter_context(nc.sbuf_tensor("sb_" + name, list(shape), dt))
            return Tile(h[tuple(slice(None) for _ in shape)], shape, es_of[dt], cs=cs)

        psb = []
        for i in range(8):
            h = st.enter_context(nc.psum_tensor(f"ps{i}", [128, 512], F32))
            psb.append(Tile(h[:, :], [128, 512], 4, cs=4096))

        ones = sb("ones", [128, 128], BF16)
        shiftm = sb("shiftm", [128, 128], F32)
        smallp = sb("smallp", [128, DEPTH * NSP], F32, cs=4)
        corep = sb("corep", [128, 2], F32, cs=4)
        epsT = sb("epsT", [128, 1], F32)
        scond = sb("scond", [128, 8, 2], BF16, cs=4)
        condS = sb("condS", [128, 8, 2], F32)
        modA = sb("modA", [128, DEPTH, 48, 2], F32, cs=8)
        gs1 = sb("gs1", [128, DEPTH, 8, 2], F32, cs=8)
        gg1 = sb("gg1", [128, DEPTH, 8, 2], F32, cs=8)
        gs2 = sb("gs2", [128, DEPTH, 8, 2], F32, cs=8)
        gg2 = sb("gg2", [128, DEPTH, 8, 2], F32, cs=8)
        dft256 = sb("dft256", [128, 2, 2, 256], BF16)
        dftc = sb("dftc", [128, 2, 128], BF16)
        spatT = sb("spatT", [128, 4, 128], BF16)
        spb = sb("spb", [128, 2, 128], F32)
        wuq = sb("wuq", [128, 2, 2, 512], BF16)
        wukv = sb("wukv", [128, 768], BF16)
        f32big = [sb(f"f32big{i}", [128, 8, 512], F32) for i in range(2)]
        sq = sb("sq", [128, 8, 512], BF16)
        sqq = sb("sqq", [128, 3, 512], BF16)
        rstd = [sb(f"rstd{i}", [128, 512], F32) for i in range(2)]
        sqrtT = [sb(f"sqrt{i}", [128, 512], F32) for i in range(1)]
        tmpf = [sb(f"tmpf{i}", [128, 512], F32) for i in range(3)]
        hcp = [sb(f"hcp{i}", [128, 512], F32) for i in range(2)]
        H = sb("H", [128, 8, 512], BF16)
        ublk = sb("ublk", [128, 2, 512], BF16)
        zext = sb("zext", [128, 2, 516], BF16)
        gbblk = sb("gbblk", [128, 2, 512], BF16)
        vptok = sb("vptok", [128, 4, 512], BF16)
        cqn = sb("cqn", [128, 2, 512], BF16)
        ckvf = sb("ckvf", [128, 512], F32)
        ckvb = sb("ckvb", [128, 512], BF16)
        krst = sb("krst", [128, 512], BF16)
        krf = sb("krf", [128, 512], F32)
        Qblk = sb("Qblk", [128, 4, 512], BF16)
        mixblk = sb("mixblk", [128, 8, 512], BF16)
        ymix = [sb(f"ymix{i}", [128, 2, 512], BF16) for i in range(2)]
        hal = sb("hal", [128, 2, 2, 16], BF16, cs=32)
        halm = sb("halm", [128, 2, 2], BF16, cs=2)
        sg = [sb(f"sg{i}", [128, 512], F32) for i in range(2)]
        convy = sg
        Pt = [sb(f"Pt{i}", [128, 512], BF16) for i in range(4)]
        Osb = [sb(f"Osb{i}", [128, 512], F32) for i in range(2)]
        rEO = [sb(f"rEO{i}", [128, 512], F32) for i in range(2)]
        UVsb = sb("UVsb", [128, 4, 512], BF16)
        wbuf = [sb(f"wbuf{i}", [128, 8, 512], BF16) for i in range(2)]
        RB = 51712
        Rh = st.enter_context(nc.sbuf_tensor("Rreg", [128, RB // 2], BF16))
        Rroot = Root(512)

        def carve(off, shape, dt):
            n = int(np.prod(shape[1:])) * es_of[dt]
            assert off % 4 == 0 and off + n <= RB, (off, n)
            ap = Rh[:, off // 2:(off + n) // 2]
            if dt == F32:
                ap = ap.bitcast(F32)
            if len(shape) == 3:
                ap = ap.rearrange("p (a b) -> p a b", a=shape[1])
            elif len(shape) == 4:
                ap = ap.rearrange("p (a b c) -> p a b c", a=shape[1], b=shape[2])
            return Tile(ap, shape, es_of[dt], root=Rroot, boff=off)

        ropeT = carve(0, [128, 2, TS], F32)
        pc_all = carve(0, [128, 32, 256], BF16)
        dftbuf = [carve(16384 + i * 8192, [128, 8, 512], BF16) for i in range(4)]
        ckv_all = carve(0, [128, 4352], BF16)
        K2 = [carve(8704 + i * 8704, [128, 4352], BF16) for i in range(2)]
        V2 = [carve(26112 + i * 8704, [128, 34, 128], BF16) for i in range(2)]
        Qh = [carve(43520 + i * 4096, [128, TS], BF16) for i in range(2)]
        Affn = carve(0, [128, 22, 512], BF16)
        wdbuf = [carve(22528 + i * 11264, [128, 22, 256], BF16) for i in range(2)]

        def apx(x):
            return x.ap if isinstance(x, View) else x

        def rd(*xs):
            return [x for x in xs if isinstance(x, View)]

        def dma(q, out, in_, **kw):
            eng = P.sp if q == "sp" else P.pool
            o, i = out.ap, in_.ap
            P.emit(eng, lambda e: e.dma_start(out=o, in_=i, **kw), [in_], [out], kind="dma")

        def act(out, in_, func, bias=None, scale=1.0):
            o, i, b, sc = out.ap, in_.ap, apx(bias), apx(scale)
            kw = {}
            if b is not None:
                kw["bias"] = b
            P.emit(P.act, lambda e: e.activation(out=o, in_=i, func=func, scale=sc, **kw),
                   [in_] + rd(bias, scale), [out])

        def tt(out, a, b, op, eng="dve"):
            E = P.dve if eng == "dve" else P.pool
            o, x, y = out.ap, a.ap, b.ap
            P.emit(E, lambda e: e.tensor_tensor(out=o, in0=x, in1=y, op=op), [a, b], [out])

        def stt(out, in0, scalar, in1, op0, op1, eng="dve"):
            E = P.dve if eng == "dve" else P.pool
            o, x, s_, y = out.ap, in0.ap, apx(scalar), in1.ap
            P.emit(E, lambda e: e.scalar_tensor_tensor(out=o, in0=x, scalar=s_, in1=y, op0=op0, op1=op1),
                   [in0, in1] + rd(scalar), [out])

        def ts(out, in0, s1, op0, s2=None, op1=None, eng="dve"):
            E = P.dve if eng == "dve" else P.pool
            o, x, a1, a2 = out.ap, in0.ap, apx(s1), apx(s2)
            if op1 is None:
                P.emit(E, lambda e: e.tensor_scalar(out=o, in0=x, scalar1=a1, scalar2=None, op0=op0),
                       [in0] + rd(s1), [out])
            else:
                P.emit(E, lambda e: e.tensor_scalar(out=o, in0=x, scalar1=a1, scalar2=a2, op0=op0, op1=op1),
                       [in0] + rd(s1, s2), [out])

        def cp(out, in_, eng="dve"):
            E = P.dve if eng == "dve" else P.pool
            o, i = out.ap, in_.ap
            P.emit(E, lambda e: e.tensor_copy(out=o, in_=i), [in_], [out])

        def recip(out, in_):
            o, i = out.ap, in_.ap
            P.emit(P.dve, lambda e: e.reciprocal(out=o, in_=i), [in_], [out])

        def memset(out, val, eng="dve"):
            E = P.dve if eng == "dve" else P.pool
            o = out.ap
            P.emit(E, lambda e: e.memset(o, val), [], [out])

        def pe(mms, final=True):
            reads, writes, seen = [], [], set()
            for (o, l, r, s0, s1) in mms:
                reads += [l, r]
                if id(o.root) not in seen or True:
                    writes.append(o)
            items = [(o.ap, l.ap, r.ap, s0, s1) for (o, l, r, s0, s1) in mms]

            def fn(e):
                ins = None
                for (o, l, r, s0, s1) in items:
                    ins = e.matmul(o, l, r, start=s0, stop=s1)
                return ins
            P.emit(P.pe, fn, reads, writes, final=True)

        rot = {"i": 0, "set": [0, 1, 2, 3, 4, 6, 7]}

        def nb():
            b = rot["set"][rot["i"] % len(rot["set"])]
            rot["i"] += 1
            return psb[b]

        SSB = psb[5]
        cnt = {"r": 0, "t": 0, "o": 0, "p": 0, "w": 0, "wd": 0, "y": 0, "c": 0, "s": 0, "d": 0, "ob": 0}

        def nxt(key, lst):
            v = lst[cnt[key] % len(lst)]
            cnt[key] += 1
            return v

        def spv(l, a, b=None):
            b = a + 1 if b is None else b
            return smallp[:, l * NSP + a:l * NSP + b]

        def rstd_from(ssv, D):
            sqv = nxt("s", sqrtT)
            act(sqv[:, :], ssv, AF.Ln, bias=epsT[:, 0:1], scale=1.0 / D)
            rv = nxt("r", rstd)
            act(rv[:, :], sqv[:, :], AF.Exp, scale=-0.5)
            return rv

        dma("sp", smallp.all(), D_in.v(smallp_d))
        dma("sp", corep.all(), D_in.v(corep_d))
        dma("sp", condS.all(), D_in.v(condT))
        dma("sp", shiftm.all(), D_in.v(shiftm_d))
        dma("sp", dft256.all(), D_in.v(dft256_d))
        dma("sp", dftc.all(), D_in.v(dftc_d))
        memset(ones.all(), 1.0)
        memset(epsT.all(), EPS)
        for i in range(2):
            memset(rEO[i].all(), 0.0)
        act(scond.all(), condS.all(), AF.Silu)
        def mod_pieces(l, pieces):
            for piece in pieces:
                wb = nxt("w", wbuf)
                dma("pool", wb.all(), D_in.v(w_ada[l, :, piece * 512:(piece + 1) * 512].rearrange("(k p) c -> p k c", p=128)))
                b = nb()
                mms = []
                for mi in range(4):
                    for k in range(8):
                        mms.append((b[:, mi * 2:mi * 2 + 2], wb[:, k, mi * 128:(mi + 1) * 128], scond[:, k, 0:2], k == 0, k == 7))
                pe(mms)
                b3 = b.ap[:, 0:8].rearrange("p (a c) -> p a c", c=2)
                for c in range(2):
                    tt(modA[:, l, piece * 4:(piece + 1) * 4, c], View(b3[:, :, c], b.root, 0, 2048),
                       spv(l, 32 + piece * 4, 36 + piece * 4), ALU.add)
            if pieces and pieces[-1] == 11:
                for c in range(2):
                    stt(gs1[:, l, :, c], modA[:, l, 8:16, c], 1.0, spv(l, 0, 8), ALU.add, ALU.mult)
                    tt(gg1[:, l, :, c], modA[:, l, 16:24, c], spv(l, 8, 16), ALU.mult)
                    stt(gs2[:, l, :, c], modA[:, l, 32:40, c], 1.0, spv(l, 16, 24), ALU.add, ALU.mult)
                    tt(gg2[:, l, :, c], modA[:, l, 40:48, c], spv(l, 24, 32), ALU.mult)

        if "mod" in stages:
            mod_pieces(0, list(range(12)))

        def load_layer_small(l):
            dma("pool", spatT.all(), D_in.v(spatT_d[l]))
            dma("sp", spb.all(), D_in.v(spb_d[l]))
            dma("pool", wuq.all(), D_in.v(w_uq_d[l]))
            dma("pool", wukv.all(), D_in.v(w_ukv_d[l]))

        def big_norm(xb, gsT, shiftcol, l, c):
            for m in range(8):
                act(sq[:, m, :], xb[:, m, :], AF.Square)
            pe([(SSB[:, :], ones[:, :], sq[:, m, :], m == 0, m == 7) for m in range(8)])
            rv = rstd_from(SSB[:, :], 1024.0)
            for m in range(8):
                t = nxt("t", tmpf)
                stt(t[:, :], xb[:, m, :], gsT[:, l, m, c:c + 1], rv[:, :], ALU.mult, ALU.mult)
                act(H[:, m, :], t[:, :], AF.Identity, bias=modA[:, l, shiftcol + m, c:c + 1])

        def post_norm_residual(osb, xb, ggT, l, c):
            pe([(SSB[:, :], ones[:, :], sq[:, m, :], m == 0, m == 7) for m in range(8)])
            rv = rstd_from(SSB[:, :], 1024.0)
            for m in range(8):
                t = nxt("t", tmpf)
                stt(t[:, :], osb[:, m, :], ggT[:, l, m, c:c + 1], rv[:, :], ALU.mult, ALU.mult)
                tt(xb[:, m, :], xb[:, m, :], t[:, :], ALU.add)

        def in_proj(l, sample, t0, dst):
            def wunit(c0, n):
                wb = nxt("w", wbuf)
                dma("pool", wb[:, :, 0:n], D_in.v(w_in_p[l, :, c0:c0 + n].rearrange("(k p) c -> p k c", p=128)))
                return wb

            def fgroup(wb, ci, M=128):
                b = nb()
                pe([(b[0:M, :], wb[:, k, ci * 128:ci * 128 + M], H[:, k, :], k == 0, k == 7) for k in range(8)])
                return b
            wb = wunit(0, 512)
            for ci in range(2):
                b = fgroup(wb, ci)
                act(ublk[:, ci, :], b[:, :], AF.Copy)
            for ci in range(2):
                b = fgroup(wb, 2 + ci)
                act(hcp[ci][:, :], b[:, :], AF.Copy)
            if not sample and plevel < 1.1:
                return
            wb = wunit(512, 512)
            for ci in range(2):
                b = fgroup(wb, ci)
                if sample:
                    tt(zext[:, ci, 0:512], hcp[ci][:, :], b[:, :], ALU.mult)
                else:
                    zd = View(zext.ap[:, ci, 0:516].rearrange("p (s t) -> p s t", s=2)[:, :, 1:257], zext.root,
                              zext[:, ci, 0:516].lo, zext[:, ci, 0:516].hi)
                    hv = View(hcp[ci].ap.rearrange("p (s t) -> p s t", s=2), hcp[ci].root, 0, 2048)
                    bv = View(b.ap.rearrange("p (s t) -> p s t", s=2), b.root, 0, 2048)
                    tt(zd, hv, bv, ALU.mult)
            for ci in range(2):
                b = fgroup(wb, 2 + ci)
                act(gbblk[:, ci, :], b[:, :], AF.Copy)
            if not sample and plevel < 1.2:
                return
            wb = wunit(1024, 512)
            bq0 = fgroup(wb, 0)
            bq1 = fgroup(wb, 1)
            bkv = fgroup(wb, 2)
            bkr = fgroup(wb, 3)
            act(sqq[:, 0, :], bq0[:, :], AF.Square)
            act(sqq[:, 1, :], bq1[:, :], AF.Square)
            act(sqq[:, 2, :], bkv[:, :], AF.Square)
            pe([(SSB[:, :], ones[:, :], sqq[:, 0, :], True, False), (SSB[:, :], ones[:, :], sqq[:, 1, :], False, True)])
            rv = rstd_from(SSB[:, :], 192.0)
            stt(cqn[:, 0, :], bq0[:, :], spv(l, 88), rv[:, :], ALU.mult, ALU.mult)
            stt(cqn[:, 1, :], bq1[:, :], spv(l, 89), rv[:, :], ALU.mult, ALU.mult)
            pe([(SSB[:, :], ones[:, :], sqq[:, 2, :], True, True)])
            rv2 = rstd_from(SSB[:, :], 128.0)
            if sample:
                stt(ckvb[:, :], bkv[:, :], spv(l, 90), rv2[:, :], ALU.mult, ALU.mult)
                dma("sp", D_xk_in.v(xk_in.ap()[0:128, t0:t0 + 512]), ckvb[:, :])
                wb3 = wunit(1536, 128)
                bkp = fgroup(wb3, 0)
                t1, t2 = nxt("t", tmpf), nxt("t", tmpf)
                tt(t1[64:96, :], bkr[64:96, :], ropeT[64:96, 0, t0:t0 + 512], ALU.mult)
                tt(t2[64:96, :], bkp[64:96, :], ropeT[64:96, 1, t0:t0 + 512], ALU.mult)
                tt(krst[64:96, :], t1[64:96, :], t2[64:96, :], ALU.add)
                dma("sp", D_xk_in.v(xk_in.ap()[128:160, t0:t0 + 512]), krst[64:96, :])
            else:
                stt(ckvf[:, :], bkv[:, :], spv(l, 90), rv2[:, :], ALU.mult, ALU.mult)
                dma("sp", D_out.fresh(ckv_o[l, :, t0:t0 + 512]), ckvf[:, :])
                cp(ckv_all[:, 0:512], ckvf[:, :])
                act(krf[64:96, :], bkr[64:96, :], AF.Copy)
                dma("sp", D_out.fresh(kr_o[l, :, t0:t0 + 512]), krf[64:96, :])
                cp(krst[64:96, :], krf[64:96, :])
            if not sample and plevel < 1.3:
                return
            for h in range(4):
                bq = nb()
                pe([(bq[:, :], wuq[:, 0, 0, h * 128:(h + 1) * 128], cqn[:, 0, :], True, False),
                    (bq[:, :], wuq[:, 0, 1, h * 128:(h + 1) * 128], cqn[:, 1, :], False, True)])
                if sample:
                    ts(Qblk[:, h, :], bq[:, :], 1.0, ALU.mult)
                else:
                    act(Qblk[:, h, :], bq[:, :], AF.Copy)
                if sample:
                    bp = nb()
                    pe([(bp[:, :], wuq[:, 1, 0, h * 128:(h + 1) * 128], cqn[:, 0, :], True, False),
                        (bp[:, :], wuq[:, 1, 1, h * 128:(h + 1) * 128], cqn[:, 1, :], False, True)])
                    t1, t2 = nxt("t", tmpf), nxt("t", tmpf)
                    tt(t1[64:96, :], bq[64:96, :], ropeT[64:96, 0, t0:t0 + 512], ALU.mult)
                    tt(t2[64:96, :], bp[64:96, :], ropeT[64:96, 1, t0:t0 + 512], ALU.mult)
                    tt(Qblk[64:96, h, :], t1[64:96, :], t2[64:96, :], ALU.add)
            if not sample and plevel < 1.4:
                return
            wb = wunit(1664, 512)
            for ti in range(4):
                b = nb()
                pe([(b[:, :], H[:, k, ti * 128:(ti + 1) * 128], wb[:, k, :], k == 0, k == 7) for k in range(8)])
                act(vptok[:, ti, :], b[:, :], AF.Copy)

        def chunk_mlp():
            for ci in range(4):
                b = nb()
                mms = []
                for pr in range(2):
                    for gi in range(2):
                        q0 = (pr * 2 + gi) * 128
                        mms.append((b[:, q0:q0 + 128], vptok[:, ci, pr * 128:(pr + 1) * 128], spatT[:, pr * 2 + gi, :], True, True))
                pe(mms)
                b4 = b.ap.rearrange("p (a g q) -> p a g q", a=2, g=2)
                for gi in range(2):
                    r0, r1 = gi * 64, gi * 64 + 64
                    t = nxt("t", tmpf)
                    tv = View(t.ap[r0:r1, 0:256].rearrange("p (a q) -> p a q", a=2), t.root, 0, 1024)
                    tt(tv, View(b4[r0:r1, :, gi, :], b.root, 0, 2048), spb[r0:r1, :, :], ALU.add)
                    tt(mixblk[r0:r1, 0:2, ci * 128:(ci + 1) * 128], tv, ublk[r0:r1, 0:2, ci * 128:(ci + 1) * 128], ALU.mult)

        def conv_seg(l, zsrc, ch, gbv, outv, L):
            y = nxt("c", convy)
            a = zsrc
            act(y[:, 0:L], zext[:, ch, a + 1:a + 1 + L], AF.Identity, bias=spv(l, 86 + ch), scale=spv(l, 80 + 2 + ch))
            stt(y[:, 0:L], zext[:, ch, a:a + L], spv(l, 80 + ch), y[:, 0:L], ALU.mult, ALU.add)
            stt(y[:, 0:L], zext[:, ch, a + 2:a + 2 + L], spv(l, 80 + 4 + ch), y[:, 0:L], ALU.mult, ALU.add)
            tt(outv, gbv, y[:, 0:L], ALU.mult)

        def attn_norm(ob, hh, outv_fn):
            osb_ = nxt("o", Osb)
            act(osb_[:, :], ob[:, :], AF.Copy)
            r = rEO[hh]
            d0, d1 = (64, 128) if hh == 0 else (0, 64)
            o0, o1 = (0, 64) if hh == 0 else (64, 128)
            recip(r[d0:d1, :], osb_[d0:d1, :])
            pe([(SSB[:, :], shiftm[:, :], r[:, :], True, True)])
            tt(outv_fn(o0, o1), osb_[o0:o1, :], SSB[o0:o1, :], ALU.mult)

        def ones_V2():
            for i in range(2):
                memset(V2[i].all(), 1.0)

        def build_V(h, hh, nkt):
            vs = 0 if hh == 0 else 64
            for k0 in range(0, nkt, 8):
                n = min(8, nkt - k0)
                b = nb()
                pe([(b[:, j * 64:(j + 1) * 64], ckv_all[:, (k0 + j) * 128:(k0 + j + 1) * 128],
                     wukv[:, 512 + h * 64:512 + (h + 1) * 64], True, True) for j in range(n)])
                bv = View(b.ap[:, 0:n * 64].rearrange("p (j v) -> p j v", v=64), b.root, 0, 2048)
                act(V2[hh][:, k0:k0 + n, vs:vs + 64], bv, AF.Copy)

        def build_Knope(h, hh, ncols):
            for c0 in range(0, ncols, 512):
                n = min(512, ncols - c0)
                b = nb()
                pe([(b[:, 0:n], wukv[:, h * 128:(h + 1) * 128], ckv_all[:, c0:c0 + n], True, True)])
                ts(K2[hh][:, c0:c0 + n], b[:, 0:n], 1.0, ALU.mult)

        def ffn_and_rest(l, c, xb, dst_store):
            big_norm(xb, gs2, 24, l, c)
            for unit in range(11):
                wb = nxt("w", wbuf)
                dma("pool", wb.all(), D_in.v(w_gu_p[l, :, unit * 512:(unit + 1) * 512].rearrange("(k p) c -> p k c", p=128)))
                for jj in range(2):
                    j = unit * 2 + jj
                    bg, bu = nb(), nb()
                    pe([(bg[:, :], wb[:, k, jj * 256:jj * 256 + 128], H[:, k, :], k == 0, k == 7) for k in range(8)])
                    pe([(bu[:, :], wb[:, k, jj * 256 + 128:jj * 256 + 256], H[:, k, :], k == 0, k == 7) for k in range(8)])
                    s_ = nxt("d", sg)
                    act(s_[:, :], bg[:, :], AF.Silu)
                    tt(Affn[:, j, :], s_[:, :], bu[:, :], ALU.mult)
            osb = f32big[1]
            for du in range(4):
                wd = nxt("wd", wdbuf)
                dma("pool", wd.all(), D_in.v(w_down[l, :, du * 256:(du + 1) * 256].rearrange("(j p) c -> p j c", p=128)))
                for mi in range(2):
                    m = du * 2 + mi
                    b = nb()
                    pe([(b[:, :], wd[:, j, mi * 128:(mi + 1) * 128], Affn[:, j, :], j == 0, j == 21) for j in range(22)])
                    act(osb[:, m, :], b[:, :], AF.Copy)
                    act(sq[:, m, :], b[:, :], AF.Square)
            post_norm_residual(osb, xb, gg2, l, c)
            for m in range(8):
                dma("sp", View(dst_store.ap[:, m, :], dst_store.root, 0, 1), xb[:, m, :])

        def out_proj_residual(l, c, xb):
            osb = f32big[1]
            for wu in range(2):
                wb = nxt("w", wbuf)
                dma("pool", wb.all(), D_in.v(w_out[l, :, wu * 512:(wu + 1) * 512].rearrange("(k p) c -> p k c", p=128)))
                for mi in range(4):
                    m = wu * 4 + mi
                    b = nb()
                    pe([(b[:, :], wb[:, k, mi * 128:(mi + 1) * 128], mixblk[:, k, :], k == 0, k == 7) for k in range(8)])
                    act(osb[:, m, :], b[:, :], AF.Copy)
                    act(sq[:, m, :], b[:, :], AF.Square)
            post_norm_residual(osb, xb, gg1, l, c)

        def load_x(xb, src):
            for m in range(8):
                dma("sp", xb[:, m, :], View(src.ap[:, m, :], src.root, 0, 1))

        pref = {}

        def get_x(xb, l, sample, t0):
            if pref.pop(("x", l, sample, t0), None):
                return
            load_x(xb, xsrc(l, sample, t0))

        def prefetch_x(xb, l, sample, t0):
            load_x(xb, xsrc(l, sample, t0))
            pref[("x", l, sample, t0)] = True

        def xsrc(l, sample, t0):
            if l == 0:
                base = xT_s if sample else xT_p
                return D_in.v(base[:, :, t0:t0 + 512].rearrange("c p t -> p c t"))
            D = D_xs_s if sample else D_xs_p
            return D.v(D.ap[:, :, t0:t0 + 512].rearrange("c p t -> p c t"))

        def xdst(l, sample, t0):
            if l == depth - 1:
                base = ys_o if sample else yp_o
                return D_out.fresh(base[:, :, t0:t0 + 512].rearrange("c p t -> p c t"))
            D = D_xs_s if sample else D_xs_p
            return D.v(D.ap[:, :, t0:t0 + 512].rearrange("c p t -> p c t"))

        def prompt_block(l, blk):
            t0 = blk * 512
            xb = f32big[0]
            get_x(xb, l, False, t0)
            big_norm(xb, gs1, 0, l, 0)
            if plevel < 0.5:
                return
            in_proj(l, False, t0, None)
            if plevel < 2:
                return
            chunk_mlp()
            for s in range(2):
                for ch in range(2):
                    conv_seg(l, s * 258, ch, gbblk[:, ch, s * 256:(s + 1) * 256], mixblk[:, 2 + ch, s * 256:(s + 1) * 256], 256)
            if plevel < 3:
                return
            for s in range(2):
                bu, bv = nb(), nb()
                for tb, bb in ((0, bu), (1, bv)):
                    mms = []
                    for pr in range(2):
                        for ti in range(2):
                            mms.append((bb[:, pr * 256:(pr + 1) * 256], vptok[:, s * 2 + ti, 256 + pr * 128:256 + (pr + 1) * 128],
                                        dft256[:, tb, ti, :], ti == 0, ti == 1))
                    pe(mms)
                act(UVsb[:, 0, :], bu[:, :], AF.Copy)
                ts(UVsb[:, 1, :], bv[:, :], 1.0, ALU.mult)
                by = nb()
                mms = []
                for pr in range(2):
                    mms.append((by[:, pr * 256:(pr + 1) * 256], dftc[:, 0, :], UVsb[:, 0, pr * 256:(pr + 1) * 256], True, False))
                    mms.append((by[:, pr * 256:(pr + 1) * 256], dftc[:, 1, :], UVsb[:, 1, pr * 256:(pr + 1) * 256], False, True))
                pe(mms)
                byv = View(by.ap.rearrange("p (a m) -> p a m", a=2), by.root, 0, 2048)
                act(mixblk[:, 4:6, s * 256:(s + 1) * 256], byv, AF.Copy)
            if plevel < 4:
                return
            rot["set"] = [0, 1, 2, 3, 4]
            ones_V2()
            for h in range(4):
                hh = h % 2
                build_Knope(h, hh, 512)
                cp(K2[hh][64:96, 0:512], krst[64:96, :])
                build_V(h, hh, 4)
                ob = psb[6 + (cnt["ob"] % 2)]
                cnt["ob"] += 1
                for s in range(2):
                    for k2 in range(2):
                        kt = s * 2 + k2
                        bs = nb()
                        pe([(bs[:, 0:256], K2[hh][:, kt * 128:(kt + 1) * 128], Qblk[:, h, s * 256:(s + 1) * 256], True, True)])
                        pt = nxt("p", Pt)
                        act(pt[:, 0:256], bs[:, 0:256], AF.Exp, scale=ATT_SCALE)
                        pe([(ob[:, s * 256:(s + 1) * 256], V2[hh][:, kt, :], pt[:, 0:256], k2 == 0, k2 == 1)],
                           final=(s == 1 and k2 == 1))
                attn_norm(ob, hh, lambda o0, o1, h=h: mixblk[o0:o1, 6 + h // 2, :])
            rot["set"] = [0, 1, 2, 3, 4, 6, 7]
            if plevel < 5:
                return
            out_proj_residual(l, 0, xb)
            if plevel < 6:
                return
            ffn_and_rest(l, 0, xb, xdst(l, False, t0))

        def sample_A_block(l, blk):
            t0 = blk * 512
            if "mod" in stages and l + 1 < depth:
                mod_pieces(l + 1, list(range(blk * 3, blk * 3 + 3)))
            xb = f32big[0]
            get_x(xb, l, True, t0)
            big_norm(xb, gs1, 0, l, 1)
            if blk < 3:
                prefetch_x(xb, l, True, t0 + 512)
            elif "P" in stages:
                prefetch_x(xb, l, False, 0)
            in_proj(l, True, t0, None)
            chunk_mlp()
            dma("sp", D_mixS.v(mixS.ap()[0:2, :, t0:t0 + 512].rearrange("c p t -> p c t")), mixblk[:, 0:2, :])
            dma("sp", D_zS.v(zS.ap()[:, :, 1 + t0:1 + t0 + 512].rearrange("c p t -> p c t")), zext[:, :, 0:512])
            dma("sp", D_gbS.v(gbS.ap()[:, :, t0:t0 + 512].rearrange("c p t -> p c t")), gbblk[:, :, :])
            dma("sp", D_QS.v(QS.ap()[:, :, t0:t0 + 512].rearrange("h p t -> p h t")), Qblk[:, :, :])
            dma("sp", D_pc_in.v(pc_in.ap()[t0:t0 + 512, :].rearrange("(a p) c -> p a c", p=128)), vptok[:, :, 256:512])
            if blk == 0:
                for ch in range(2):
                    dma("sp", D_xk_in.v(xk_in.ap()[160 + ch, :].rearrange("(p c) -> p c", c=16)), zext[:, ch, 0:16])
            if blk == 3:
                for ch in range(2):
                    dma("sp", D_xk_in.v(xk_in.ap()[162 + ch, :].rearrange("(p c) -> p c", c=16)), zext[:, ch, 496:512])

        def exchange():
            a_in, a_out = xk_in.ap().opt(), xk_out.ap().opt()
            P.emit(P.pool, lambda e: e.collective_compute("AllGather", ALU.bypass, replica_groups=groups,
                                                          ins=[a_in], outs=[a_out]),
                   [D_xk_in.v()], [D_xk_out.v()], kind="cc")
            b_in, b_out = pc_in.ap().opt(), pc_out.ap().opt()
            P.emit(P.pool, lambda e: e.collective_compute("AllGather", ALU.bypass, replica_groups=groups,
                                                          ins=[b_in], outs=[b_out]),
                   [D_pc_in.v()], [D_pc_out.v()], kind="cc")

        def sample_mix(l):
            xo = xk_out.ap()
            for ch in range(2):
                dma("sp", hal[:, ch, 0, :], D_xk_out.v(xo[162 + ch, :].rearrange("(p c) -> p c", c=16)))
                dma("sp", hal[:, ch, 1, :], D_xk_out.v(xo[164 + 160 + ch, :].rearrange("(p c) -> p c", c=16)))
            ts(halm[:, :, 0], hal[:, :, 0, 15], corep[:, 0:1], ALU.mult)
            ts(halm[:, :, 1], hal[:, :, 1, 0], corep[:, 1:2], ALU.mult)
            def conv_block(blk):
                t0 = blk * 512
                if blk == 0:
                    dma("sp", zext[:, :, 1:514], D_zS.v(zS.ap()[:, :, 1:514].rearrange("c p t -> p c t")))
                    for ch in range(2):
                        cp(zext[:, ch, 0:1], halm[:, ch, 0:1])
                elif blk == 3:
                    dma("sp", zext[:, :, 0:513], D_zS.v(zS.ap()[:, :, t0:t0 + 513].rearrange("c p t -> p c t")))
                    for ch in range(2):
                        cp(zext[:, ch, 513:514], halm[:, ch, 1:2])
                else:
                    dma("sp", zext[:, :, 0:514], D_zS.v(zS.ap()[:, :, t0:t0 + 514].rearrange("c p t -> p c t")))
                dma("sp", gbblk[:, :, :], D_gbS.v(gbS.ap()[:, :, t0:t0 + 512].rearrange("c p t -> p c t")))
                ym = nxt("y", ymix)
                for ch in range(2):
                    conv_seg(l, 0, ch, gbblk[:, ch, :], ym[:, ch, :], 512)
                dma("sp", D_mixS.v(mixS.ap()[2:4, :, t0:t0 + 512].rearrange("c p t -> p c t")), ym[:, :, :])
            dma("sp", pc_all.all(), D_pc_out.v(pc_out.ap().rearrange("(a p) c -> p a c", p=128)))
            rot["set"] = [0, 1, 2, 3]
            acc = [psb[4], psb[5], psb[6], psb[7]]
            for mb in range(4):
                for ng in range(4):
                    ct, st_ = nxt("d", dftbuf), nxt("d", dftbuf)
                    for tb, buf in ((0, ct), (1, st_)):
                        dma("sp", buf.all(), D_in.v(dft4096_d[tb, ng * 1024:(ng + 1) * 1024, mb * 512:(mb + 1) * 512]
                                                     .rearrange("(a p) m -> p a m", p=128)))
                    mms = []
                    for a in range(8):
                        nt = ng * 8 + a
                        for pr in range(2):
                            mms.append((acc[pr][:, :], pc_all[:, nt, pr * 128:(pr + 1) * 128], ct[:, a, :], nt == 0, nt == 31))
                            mms.append((acc[2 + pr][:, :], pc_all[:, nt, pr * 128:(pr + 1) * 128], st_[:, a, :], nt == 0, nt == 31))
                    pe(mms, final=(ng == 3))
                for i in range(4):
                    if i % 2 == 0:
                        act(UVsb[:, i, :], acc[i][:, :], AF.Copy)
                    else:
                        ts(UVsb[:, i, :], acc[i][:, :], 1.0, ALU.mult)
                ym = nxt("y", ymix)
                for pr in range(2):
                    by = nb()
                    pe([(by[:, :], dftc[:, 0, :], UVsb[:, pr, :], True, False), (by[:, :], dftc[:, 1, :], UVsb[:, 2 + pr, :], False, True)])
                    act(ym[:, pr, :], by[:, :], AF.Copy)
                dma("sp", D_mixS.v(mixS.ap()[4:6, :, mb * 512:(mb + 1) * 512].rearrange("c p t -> p c t")), ym[:, :, :])
                conv_block(mb)
            rot["set"] = [0, 1, 2, 3, 4]
            ones_V2()
            dma("sp", ckv_all[:, 0:2048], D_xk_out.v(xo[0:128, :]))
            dma("sp", ckv_all[:, 2048:4096], D_xk_out.v(xo[164:292, :]))
            dma("pool", ckv_all[:, 4096:4352], D_in.v(cacheT_ckv[l]))
            for h in range(4):
                hh = h % 2
                build_Knope(h, hh, 4352)
                dma("sp", K2[hh][64:96, 0:2048], D_xk_out.v(xo[128:160, :]))
                dma("sp", K2[hh][64:96, 2048:4096], D_xk_out.v(xo[292:324, :]))
                dma("pool", K2[hh][64:96, 4096:4352], D_in.v(cacheT_kr[l]))
                build_V(h, hh, 34)
                qh = nxt("c", Qh)
                dma("sp", qh[:, :], D_QS.v(QS.ap()[h, :, :]))
                for qb in range(4):
                    ob = psb[6 + (cnt["ob"] % 2)]
                    cnt["ob"] += 1
                    pts = {}

                    def s_step(kt):
                        bs = nb()
                        pe([(bs[:, :], K2[hh][:, kt * 128:(kt + 1) * 128], qh[:, qb * 512:(qb + 1) * 512], True, True)])
                        pt = nxt("p", Pt)
                        act(pt[:, :], bs[:, :], AF.Exp, scale=ATT_SCALE)
                        pts[kt] = pt
                    s_step(0)
                    s_step(1)
                    for kt in range(34):
                        if kt + 2 < 34:
                            s_step(kt + 2)
                        pe([(ob[:, :], V2[hh][:, kt, :], pts.pop(kt)[:, :], kt == 0, kt == 33)], final=(kt == 33))
                    ym = nxt("y", ymix)
                    attn_norm(ob, hh, lambda o0, o1: ym[o0:o1, 0, :])
                    o0, o1 = (0, 64) if hh == 0 else (64, 128)
                    dma("sp", D_mixS.v(mixS.ap()[6 + h // 2, o0:o1, qb * 512:(qb + 1) * 512]), ym[o0:o1, 0, :])
            rot["set"] = [0, 1, 2, 3, 4, 6, 7]

        def sample_F_block(l, blk):
            t0 = blk * 512
            xb = f32big[0]
            load_x(xb, xsrc(l, True, t0))
            if not pref.pop(("m", l, blk), None):
                dma("sp", mixblk.all(), D_mixS.v(mixS.ap()[:, :, t0:t0 + 512].rearrange("c p t -> p c t")))
            out_proj_residual(l, 1, xb)
            if blk < 3:
                dma("sp", mixblk.all(), D_mixS.v(mixS.ap()[:, :, t0 + 512:t0 + 1024].rearrange("c p t -> p c t")))
                pref[("m", l, blk + 1)] = True
            ffn_and_rest(l, 1, xb, xdst(l, True, t0))

        memset(zext.all(), 0.0)
        for l in range(depth):
            load_layer_small(l)
            dma("sp", ropeT.all(), D_in.v(ropeT_d.rearrange("a p t -> p a t")))
            if "SA" in stages:
                for blk in range(4):
                    sample_A_block(l, blk)
            memset(zext.all(), 0.0)
            if "P" in stages:
                prompt_block(l, 0)
            if "EX" in stages:
                exchange()
            if "P" in stages:
                prompt_block(l, 1)
            if "SM" in stages:
                sample_mix(l)
            if "SF" in stages:
                for blk in range(4):
                    sample_F_block(l, blk)

        fin = []
        for q in (P.sp, P.pool):
            for name, n in q.slots:
                if n > 0 and P.sp.waited.get(name, 0) < 16 * n:
                    fin.append((name, 16 * n))
        P.sp.ops.append((fin, None, None))

        semnames = ["pe", "act", "dve", "pool", "cc"] + [s_[0] for q in (P.sp, P.pool) for s_ in q.slots]
        sems = {n: st.enter_context(nc.semaphore(n)) for n in semnames}
        block = st.enter_context(nc.Block())

        def replay(eng):
            def run(e):
                for waits, fn, inc in eng.ops:
                    for k, v in waits:
                        e.wait_ge(sems[k], v)
                    if fn is None:
                        continue
                    ins = fn(e)
                    if inc is not None:
                        if inc[1] is None:
                            ins.then_inc(sems[inc[0]])
                        else:
                            ins.then_inc(sems[inc[0]], inc[1])
            return run

        block.tensor(replay(P.pe))
        block.scalar(replay(P.act))
        block.vector(replay(P.dve))
        block.gpsimd(replay(P.pool))
        block.sync(replay(P.sp))
    return nc


_CACHE = {}


def _consts():
    if "c" in _CACHE:
        return _CACHE["c"]
    bf = ml_dtypes.bfloat16
    n = np.arange(256)
    ang = 2 * np.pi * ((n[:, None] * n[None, :]) % 256) / 256.0
    C, S = np.cos(ang) / 16.0, np.sin(ang) / 16.0
    dft256 = np.stack([C, S], 0).reshape(2, 2, 128, 256).transpose(2, 0, 1, 3).astype(bf)
    ch = np.arange(64)
    angc = 2 * np.pi * ((ch[:, None] * ch[None, :]) % 64) / 64.0
    Cc, Sc = np.cos(angc) / 8.0, np.sin(angc) / 8.0
    CcB = np.zeros((128, 128)); nScB = np.zeros((128, 128))
    for g in range(2):
        CcB[g * 64:(g + 1) * 64, g * 64:(g + 1) * 64] = Cc
        nScB[g * 64:(g + 1) * 64, g * 64:(g + 1) * 64] = -Sc
    dftc = np.stack([CcB, nScB], 1).astype(bf)
    nn = np.arange(4096, dtype=np.int64)
    dft4096 = []
    for h in range(2):
        mm = h * 2048 + np.arange(2048, dtype=np.int64)
        k = (nn[:, None] * mm[None, :]) % 4096
        a = 2 * np.pi * k / 4096.0
        dft4096.append(np.stack([np.cos(a) / 64.0, np.sin(a) / 64.0], 0).astype(bf))
    shiftm = np.zeros((128, 128), np.float32)
    for m in range(128):
        shiftm[(m + 64) % 128, m] = 1.0
    rope = []
    inv = (10000.0 ** (-np.arange(0, 16, 2, dtype=np.float32) / 16.0)).astype(np.float32)
    for h in range(2):
        m = h * 2048 + np.arange(2048)
        pos = np.stack([(m // 64).astype(np.float32), (m % 64).astype(np.float32)], 0)
        tab = np.zeros((2, 128, 2048), np.float32)
        for r in range(32):
            a, b, j = r // 16, (r % 16) // 8, r % 8
            angr = (pos[a] * inv[j]).astype(np.float32)
            tab[0, 64 + r] = np.cos(angr)
            tab[1, 64 + r] = (-np.sin(angr)) if b == 0 else np.sin(angr)
        rope.append(tab)
    _CACHE["c"] = dict(dft256=dft256, dftc=dftc, dft4096=dft4096, shiftm=shiftm, rope=rope)
    return _CACHE["c"]


def kernel(x_prompt, x_sample, cache_ckv, cache_krope, c, c_ctx, w_ada, b_ada,
           g_pre_mix, g_post_mix, g_pre_ffn, g_post_ffn, w_in, spat_w, spat_b,
           conv_w, conv_b, g_q_lora, w_uq, g_kv_lora, w_ukv, w_out, w_gate_up, w_down):
    f = lambda a: np.ascontiguousarray(np.asarray(a, dtype=np.float32))
    x_prompt, x_sample, cache_ckv, cache_krope, c, c_ctx = map(f, (x_prompt, x_sample, cache_ckv, cache_krope, c, c_ctx))
    w_ada, b_ada, w_in, spat_w, spat_b, conv_w, conv_b = map(f, (w_ada, b_ada, w_in, spat_w, spat_b, conv_w, conv_b))
    g_pre_mix, g_post_mix, g_pre_ffn, g_post_ffn = map(f, (g_pre_mix, g_post_mix, g_pre_ffn, g_post_ffn))
    g_q_lora, w_uq, g_kv_lora, w_ukv, w_out, w_gate_up, w_down = map(f, (g_q_lora, w_uq, g_kv_lora, w_ukv, w_out, w_gate_up, w_down))
    K = _consts()
    L = DEPTH
    perm = np.array([(r // 16) * 16 + (1 - (r % 16) // 8) * 8 + r % 8 for r in range(32)])
    w_in_p = np.zeros((L, 1024, 2176), np.float32)
    w_in_p[:, :, 0:256] = w_in[:, :, 0:256]
    w_in_p[:, :, 256:512] = w_in[:, :, 512:768]
    w_in_p[:, :, 512:768] = w_in[:, :, 1024:1280]
    w_in_p[:, :, 768:1024] = w_in[:, :, 768:1024]
    w_in_p[:, :, 1024:1216] = w_in[:, :, 1536:1728]
    w_in_p[:, :, 1280:1408] = w_in[:, :, 1728:1856]
    w_in_p[:, :, 1408 + 64:1408 + 96] = w_in[:, :, 1856:1888]
    w_in_p[:, :, 1536 + 64:1536 + 96] = w_in[:, :, 1856 + perm]
    w_in_p[:, :, 1664:1920] = w_in[:, :, 256:512]
    w_in_p[:, :, 1920:2176] = w_in[:, :, 1280:1536]
    w_gu_p = np.ascontiguousarray(
        np.stack([w_gate_up[:, :, :2816].reshape(L, 1024, 22, 128), w_gate_up[:, :, 2816:].reshape(L, 1024, 22, 128)], 3)
        .reshape(L, 1024, 5632))
    spatT = np.ascontiguousarray(spat_w.transpose(0, 3, 1, 2))
    spb_bc = np.zeros((L, 128, 2, 128), np.float32)
    for pr in range(2):
        spb_bc[:, 0:64, pr, :] = spat_b[:, 2 * pr, None, :]
        spb_bc[:, 64:128, pr, :] = spat_b[:, 2 * pr + 1, None, :]
    w_uq_p = np.zeros((L, 128, 2, 2, 512), np.float32)
    for h in range(4):
        w_uq_p[:, :, 0, 0, h * 128:h * 128 + 96] = w_uq[:, 0:128, h * 96:(h + 1) * 96]
        w_uq_p[:, 0:64, 0, 1, h * 128:h * 128 + 96] = w_uq[:, 128:192, h * 96:(h + 1) * 96]
        src = h * 96 + 64 + perm
        w_uq_p[:, :, 1, 0, h * 128 + 64:h * 128 + 96] = w_uq[:, 0:128, src]
        w_uq_p[:, 0:64, 1, 1, h * 128 + 64:h * 128 + 96] = w_uq[:, 128:192, src]
    w_ukv_p = np.zeros((L, 128, 768), np.float32)
    _kv = w_ukv.reshape(L, 128, 4, 2, 64)
    for h in range(4):
        w_ukv_p[:, :, h * 128:h * 128 + 64] = _kv[:, :, h, 0, :]
        w_ukv_p[:, :, 512 + h * 64:512 + (h + 1) * 64] = _kv[:, :, h, 1, :]
    smallp = np.zeros((128, L * NSP), np.float32)
    v8 = lambda v: v.reshape(8, 128).T
    for l in range(L):
        o = l * NSP
        smallp[:, o + 0:o + 8] = v8(g_pre_mix[l])
        smallp[:, o + 8:o + 16] = v8(g_post_mix[l])
        smallp[:, o + 16:o + 24] = v8(g_pre_ffn[l])
        smallp[:, o + 24:o + 32] = v8(g_post_ffn[l])
        smallp[:, o + 32:o + 80] = b_ada[l].reshape(48, 128).T
        for k in range(3):
            smallp[:, o + 80 + 2 * k:o + 82 + 2 * k] = conv_w[l, k].reshape(2, 128).T
        smallp[:, o + 86:o + 88] = conv_b[l].reshape(2, 128).T
        smallp[:, o + 88] = g_q_lora[l, 0:128]
        smallp[0:64, o + 89] = g_q_lora[l, 128:192]
        smallp[:, o + 90] = g_kv_lora[l]
    shared = dict(w_ada=w_ada, w_in_p=w_in_p, w_out=w_out, w_gu_p=w_gu_p, w_down=w_down, spatT=spatT, spb_bc=spb_bc,
                  w_uq_p=w_uq_p, w_ukv_p=w_ukv_p, smallp=smallp, dft256=K["dft256"], dftc=K["dftc"], shiftm=K["shiftm"])
    in_maps = []
    for i in range(8):
        b, h = i // 2, i % 2
        xp = x_prompt[4 * i:4 * i + 4].reshape(TP, 1024)
        xs = x_sample[b, h * TS:(h + 1) * TS]
        cond = np.stack([c_ctx, c[b]], -1)
        corep = np.zeros((128, 2), np.float32)
        corep[:, 0] = float(h)
        corep[:, 1] = float(1 - h)
        m = dict(shared)
        m.update(
            xT_p=np.ascontiguousarray(xp.T.reshape(8, 128, TP)),
            xT_s=np.ascontiguousarray(xs.T.reshape(8, 128, TS)),
            condT=np.ascontiguousarray(cond.reshape(8, 128, 2).transpose(1, 0, 2)),
            cacheT_ckv=np.ascontiguousarray(cache_ckv[b].transpose(0, 2, 1)),
            cacheT_kr=np.ascontiguousarray(cache_krope[b].transpose(0, 2, 1)),
            corep=corep, ropeT=K["rope"][h], dft4096=K["dft4096"][h],
        )
        in_maps.append(m)
    if _CACHE.get("prep_only"):
        return in_maps
    if "nc" not in _CACHE:
        _CACHE["nc"] = build_program()
    res = run_bass_kernel_spmd(_CACHE["nc"], in_maps, core_ids=list(range(8)))
    return _post(res.results)


def _post(results):
    class _R:
        pass
    res = _R()
    res.results = results
    y_prompt = np.zeros((32, 256, 1024), np.float32)
    y_sample = np.zeros((4, 4096, 1024), np.float32)
    state_ckv = np.zeros((32, DEPTH, 256, 128), np.float32)
    state_kr = np.zeros((32, DEPTH, 256, 32), np.float32)
    for i in range(8):
        r = res.results[i]
        b, h = i // 2, i % 2
        y_prompt[4 * i:4 * i + 4] = np.asarray(r["yp_o"]).reshape(1024, TP).T.reshape(4, 256, 1024)
        y_sample[b, h * TS:(h + 1) * TS] = np.asarray(r["ys_o"]).reshape(1024, TS).T
        state_ckv[4 * i:4 * i + 4] = np.asarray(r["ckv_o"]).transpose(2, 0, 1).reshape(4, 256, DEPTH, 128).transpose(0, 2, 1, 3)
        state_kr[4 * i:4 * i + 4] = np.asarray(r["kr_o"]).transpose(2, 0, 1).reshape(4, 256, DEPTH, 32).transpose(0, 2, 1, 3)
    return (y_prompt, y_sample, state_ckv, state_kr)
```

```python
import numpy as np
import ml_dtypes
from contextlib import ExitStack
import concourse.bass as bass
import concourse.mybir as mybir
from concourse.bass_utils import run_bass_kernel_spmd

F32 = mybir.dt.float32
BF16 = mybir.dt.bfloat16
AF = mybir.ActivationFunctionType
ALU = mybir.AluOpType

DEPTH = 4
NSP = 91
TP, TS = 1024, 2048
EPS = 1e-6
ATT_SCALE = float((64 + 32) ** -0.5)
NSLOT = 14


class Cell:
    __slots__ = ("w", "r")

    def __init__(s):
        s.w = None
        s.r = {}


class Root:
    def __init__(s, cs):
        s.cs = cs
        s.cells = {}

    def get(s, lo, hi):
        out = []
        for i in range(lo // s.cs, (hi - 1) // s.cs + 1):
            c = s.cells.get(i)
            if c is None:
                c = s.cells[i] = Cell()
            out.append(c)
        return out


class View:
    __slots__ = ("ap", "root", "lo", "hi")

    def __init__(s, ap, root, lo, hi):
        s.ap, s.root, s.lo, s.hi = ap, root, lo, hi

    def cells(s):
        return s.root.get(s.lo, s.hi)


class Tile:
    def __init__(s, ap, shape, es, root=None, boff=0, cs=512):
        s.ap, s.shape, s.es = ap, list(shape), es
        s.root = root if root is not None else Root(cs)
        s.boff = boff
        st = [1] * len(shape)
        for i in range(len(shape) - 2, 0, -1):
            st[i] = st[i + 1] * shape[i + 1]
        s.st = st

    def __getitem__(s, idx):
        if not isinstance(idx, tuple):
            idx = (idx,)
        idx = tuple(idx) + (slice(None),) * (len(s.shape) - len(idx))
        lo = 0
        hi = 0
        for d in range(1, len(s.shape)):
            ix = idx[d]
            if isinstance(ix, int):
                a, b = ix, ix + 1
            else:
                a = 0 if ix.start is None else ix.start
                b = s.shape[d] if ix.stop is None else ix.stop
            assert 0 <= a < b <= s.shape[d], (idx, s.shape)
            lo += a * s.st[d]
            hi += (b - 1) * s.st[d]
        return View(s.ap[idx], s.root, s.boff + lo * s.es, s.boff + (hi + 1) * s.es)

    def all(s):
        return s[tuple(slice(None) for _ in s.shape)]


class DTile:
    def __init__(s, ap):
        s.ap = ap
        s.root = Root(1 << 40)

    def v(s, ap=None):
        return View(s.ap if ap is None else ap, s.root, 0, 1)

    def fresh(s, ap):
        return View(ap, Root(1 << 40), 0, 1)


class Eng:
    def __init__(s, name, is_pe=False):
        s.name = name
        s.is_pe = is_pe
        s.ops = []
        s.count = 0
        s.waited = {}
        s.slots = []
        s.slot_i = 0


class Prog:
    def __init__(s):
        s.pe = Eng("pe", True)
        s.act = Eng("act")
        s.dve = Eng("dve")
        s.pool = Eng("pool")
        s.sp = Eng("sp")
        s.cc_count = 0
        for q in (s.sp, s.pool):
            q.slots = [[f"{q.name}_d{i}", 0] for i in range(NSLOT)]

    def emit(s, eng, fn, reads, writes, kind="op", final=True):
        deps = {}

        def add(sig):
            if sig is not None and deps.get(sig[0], 0) < sig[1]:
                deps[sig[0]] = sig[1]

        rc = [c for v in reads for c in v.cells()]
        wc = [c for v in writes for c in v.cells()]
        for c in rc:
            add(c.w)
        for c in wc:
            add(c.w)
            for k, val in c.r.items():
                add((k, val))
        waits = []
        for k, v in deps.items():
            if eng.is_pe and k == eng.name:
                continue
            if eng.waited.get(k, 0) >= v:
                continue
            eng.waited[k] = v
            waits.append((k, v))
        if kind == "dma":
            slot = eng.slots[eng.slot_i]
            eng.slot_i = (eng.slot_i + 1) % len(eng.slots)
            prev = 16 * slot[1]
            if prev > 0 and eng.waited.get(slot[0], 0) < prev:
                eng.waited[slot[0]] = prev
                waits.append((slot[0], prev))
            slot[1] += 1
            sig = (slot[0], 16 * slot[1])
            inc = (slot[0], 16)
        elif kind == "cc":
            s.cc_count += 1
            sig = ("cc", s.cc_count)
            inc = ("cc", None)
        else:
            sig = (eng.name, eng.count + 1)
            if final:
                eng.count += 1
                inc = (eng.name, 1)
            else:
                inc = None
        eng.ops.append((waits, fn, inc))
        for c in rc:
            if c.r.get(sig[0], 0) < sig[1]:
                c.r[sig[0]] = sig[1]
        for c in wc:
            c.w = sig
            c.r = {}
        return sig


def build_program(depth=DEPTH, groups=None, stages="mod,SA,EX,P,SM,SF"):
    groups = groups or [[0, 1], [2, 3], [4, 5], [6, 7]]
    stages = set(stages.split(","))
    plevel = 9
    for t_ in list(stages):
        if t_.startswith("P:"):
            plevel = float(t_[2:])
            stages.add("P")
    nc = bass.Bass("TRN2", target_bir_lowering=False)
    P = Prog()

    def din(name, shape, dt=F32):
        return nc.dram_tensor(name, list(shape), dt, kind="ExternalInput").ap()

    def dout(name, shape, dt=F32):
        return nc.dram_tensor(name, list(shape), dt, kind="ExternalOutput").ap()

    def dscr(name, shape, dt=BF16):
        return nc.dram_tensor(name, list(shape), dt)

    xT_p = din("xT_p", [8, 128, TP])
    xT_s = din("xT_s", [8, 128, TS])
    condT = din("condT", [128, 8, 2])
    cacheT_ckv = din("cacheT_ckv", [DEPTH, 128, 256])
    cacheT_kr = din("cacheT_kr", [DEPTH, 32, 256])
    w_ada = din("w_ada", [DEPTH, 1024, 6144])
    w_in_p = din("w_in_p", [DEPTH, 1024, 2176])
    w_out = din("w_out", [DEPTH, 1024, 1024])
    w_gu_p = din("w_gu_p", [DEPTH, 1024, 5632])
    w_down = din("w_down", [DEPTH, 2816, 1024])
    spatT_d = din("spatT", [DEPTH, 128, 4, 128])
    spb_d = din("spb_bc", [DEPTH, 128, 2, 128])
    w_uq_d = din("w_uq_p", [DEPTH, 128, 2, 2, 512])
    w_ukv_d = din("w_ukv_p", [DEPTH, 128, 768])
    smallp_d = din("smallp", [128, DEPTH * NSP])
    corep_d = din("corep", [128, 2])
    ropeT_d = din("ropeT", [2, 128, TS])
    dft256_d = din("dft256", [128, 2, 2, 256], BF16)
    dft4096_d = din("dft4096", [2, 4096, 2048], BF16)
    dftc_d = din("dftc", [128, 2, 128], BF16)
    shiftm_d = din("shiftm", [128, 128])

    yp_o = dout("yp_o", [8, 128, TP])
    ys_o = dout("ys_o", [8, 128, TS])
    ckv_o = dout("ckv_o", [DEPTH, 128, TP])
    kr_o = dout("kr_o", [DEPTH, 32, TP])

    xs_p = dscr("xs_p", [8, 128, TP], F32)
    xs_s = dscr("xs_s", [8, 128, TS], F32)
    zS = dscr("zS", [2, 128, TS + 2])
    gbS = dscr("gbS", [2, 128, TS])
    QS = dscr("QS", [4, 128, TS])
    mixS = dscr("mixS", [8, 128, TS])
    xk_in = dscr("xk_in", [164, TS])
    xk_out = dscr("xk_out", [328, TS])
    pc_in = dscr("pc_in", [TS, 256])
    pc_out = dscr("pc_out", [2 * TS, 256])
    D_xs_p, D_xs_s = DTile(xs_p.ap()), DTile(xs_s.ap())
    D_zS, D_gbS, D_QS, D_mixS = DTile(zS.ap()), DTile(gbS.ap()), DTile(QS.ap()), DTile(mixS.ap())
    D_xk_in, D_xk_out = DTile(xk_in.ap()), DTile(xk_out.ap())
    D_pc_in, D_pc_out = DTile(pc_in.ap()), DTile(pc_out.ap())
    D_in = DTile(xT_p)
    D_out = DTile(yp_o)

    es_of = {F32: 4, BF16: 2}

    with ExitStack() as st:
        def sb(name, shape, dt, cs=512):
            h = st.enter_context(nc.sbuf_tensor("sb_" + name, list(shape), dt))
            return Tile(h[tuple(slice(None) for _ in shape)], shape, es_of[dt], cs=cs)

        psb = []
        for i in range(8):
            h = st.enter_context(nc.psum_tensor(f"ps{i}", [128, 512], F32))
            psb.append(Tile(h[:, :], [128, 512], 4, cs=4096))

        ones = sb("ones", [128, 128], BF16)
        shiftm = sb("shiftm", [128, 128], F32)
        smallp = sb("smallp", [128, DEPTH * NSP], F32, cs=4)
        corep = sb("corep", [128, 2], F32, cs=4)
        epsT = sb("epsT", [128, 1], F32)
        scond = sb("scond", [128, 8, 2], BF16, cs=4)
        condS = sb("condS", [128, 8, 2], F32)
        modA = sb("modA", [128, DEPTH, 48, 2], F32, cs=8)
        gs1 = sb("gs1", [128, DEPTH, 8, 2], F32, cs=8)
        gg1 = sb("gg1", [128, DEPTH, 8, 2], F32, cs=8)
        gs2 = sb("gs2", [128, DEPTH, 8, 2], F32, cs=8)
        gg2 = sb("gg2", [128, DEPTH, 8, 2], F32, cs=8)
        dft256 = sb("dft256", [128, 2, 2, 256], BF16)
        dftc = sb("dftc", [128, 2, 128], BF16)
        spatT = sb("spatT", [128, 4, 128], BF16)
        spb = sb("spb", [128, 2, 128], F32)
        wuq = sb("wuq", [128, 2, 2, 512], BF16)
        wukv = sb("wukv", [128, 768], BF16)
        f32big = [sb(f"f32big{i}", [128, 8, 512], F32) for i in range(2)]
        sq = sb("sq", [128, 8, 512], BF16)
        sqq = sb("sqq", [128, 3, 512], BF16)
        rstd = [sb(f"rstd{i}", [128, 512], F32) for i in range(2)]
        sqrtT = [sb(f"sqrt{i}", [128, 512], F32) for i in range(1)]
        tmpf = [sb(f"tmpf{i}", [128, 512], F32) for i in range(3)]
        hcp = [sb(f"hcp{i}", [128, 512], F32) for i in range(2)]
        H = sb("H", [128, 8, 512], BF16)
        ublk = sb("ublk", [128, 2, 512], BF16)
        zext = sb("zext", [128, 2, 516], BF16)
        gbblk = sb("gbblk", [128, 2, 512], BF16)
        vptok = sb("vptok", [128, 4, 512], BF16)
        cqn = sb("cqn", [128, 2, 512], BF16)
        ckvf = sb("ckvf", [128, 512], F32)
        ckvb = sb("ckvb", [128, 512], BF16)
        krst = sb("krst", [128, 512], BF16)
        krf = sb("krf", [128, 512], F32)
        Qblk = sb("Qblk", [128, 4, 512], BF16)
        mixblk = sb("mixblk", [128, 8, 512], BF16)
        ymix = [sb(f"ymix{i}", [128, 2, 512], BF16) for i in range(2)]
        hal = sb("hal", [128, 2, 2, 16], BF16, cs=32)
        halm = sb("halm", [128, 2, 2], BF16, cs=2)
        sg = [sb(f"sg{i}", [128, 512], F32) for i in range(2)]
        convy = sg
        Pt = [sb(f"Pt{i}", [128, 512], BF16) for i in range(4)]
        Osb = [sb(f"Osb{i}", [128, 512], F32) for i in range(2)]
        rEO = [sb(f"rEO{i}", [128, 512], F32) for i in range(2)]
        UVsb = sb("UVsb", [128, 4, 512], BF16)
        wbuf = [sb(f"wbuf{i}", [128, 8, 512], BF16) for i in range(2)]
        RB = 51712
        Rh = st.enter_context(nc.sbuf_tensor("Rreg", [128, RB // 2], BF16))
        Rroot = Root(512)

        def carve(off, shape, dt):
            n = int(np.prod(shape[1:])) * es_of[dt]
            assert off % 4 == 0 and off + n <= RB, (off, n)
            ap = Rh[:, off // 2:(off + n) // 2]
            if dt == F32:
                ap = ap.bitcast(F32)
            if len(shape) == 3:
                ap = ap.rearrange("p (a b) -> p a b", a=shape[1])
            elif len(shape) == 4:
                ap = ap.rearrange("p (a b c) -> p a b c", a=shape[1], b=shape[2])
            return Tile(ap, shape, es_of[dt], root=Rroot, boff=off)

        ropeT = carve(0, [128, 2, TS], F32)
        pc_all = carve(0, [128, 32, 256], BF16)
        dftbuf = [carve(16384 + i * 8192, [128, 8, 512], BF16) for i in range(4)]
        ckv_all = carve(0, [128, 4352], BF16)
        K2 = [carve(8704 + i * 8704, [128, 4352], BF16) for i in range(2)]
        V2 = [carve(26112 + i * 8704, [128, 34, 128], BF16) for i in range(2)]
        Qh = [carve(43520 + i * 4096, [128, TS], BF16) for i in range(2)]
        Affn = carve(0, [128, 22, 512], BF16)
        wdbuf = [carve(22528 + i * 11264, [128, 22, 256], BF16) for i in range(2)]

        def apx(x):
            return x.ap if isinstance(x, View) else x

        def rd(*xs):
            return [x for x in xs if isinstance(x, View)]

        def dma(q, out, in_, **kw):
            eng = P.sp if q == "sp" else P.pool
            o, i = out.ap, in_.ap
            P.emit(eng, lambda e: e.dma_start(out=o, in_=i, **kw), [in_], [out], kind="dma")

        def act(out, in_, func, bias=None, scale=1.0):
            o, i, b, sc = out.ap, in_.ap, apx(bias), apx(scale)
            kw = {}
            if b is not None:
                kw["bias"] = b
            P.emit(P.act, lambda e: e.activation(out=o, in_=i, func=func, scale=sc, **kw),
                   [in_] + rd(bias, scale), [out])

        def tt(out, a, b, op, eng="dve"):
            E = P.dve if eng == "dve" else P.pool
            o, x, y = out.ap, a.ap, b.ap
            P.emit(E, lambda e: e.tensor_tensor(out=o, in0=x, in1=y, op=op), [a, b], [out])

        def stt(out, in0, scalar, in1, op0, op1, eng="dve"):
            E = P.dve if eng == "dve" else P.pool
            o, x, s_, y = out.ap, in0.ap, apx(scalar), in1.ap
            P.emit(E, lambda e: e.scalar_tensor_tensor(out=o, in0=x, scalar=s_, in1=y, op0=op0, op1=op1),
                   [in0, in1] + rd(scalar), [out])

        def ts(out, in0, s1, op0, s2=None, op1=None, eng="dve"):
            E = P.dve if eng == "dve" else P.pool
            o, x, a1, a2 = out.ap, in0.ap, apx(s1), apx(s2)
            if op1 is None:
                P.emit(E, lambda e: e.tensor_scalar(out=o, in0=x, scalar1=a1, scalar2=None, op0=op0),
                       [in0] + rd(s1), [out])
            else:
                P.emit(E, lambda e: e.tensor_scalar(out=o, in0=x, scalar1=a1, scalar2=a2, op0=op0, op1=op1),
                       [in0] + rd(s1, s2), [out])

        def cp(out, in_, eng="dve"):
            E = P.dve if eng == "dve" else P.pool
            o, i = out.ap, in_.ap
            P.emit(E, lambda e: e.tensor_copy(out=o, in_=i), [in_], [out])

        def recip(out, in_):
            o, i = out.ap, in_.ap
            P.emit(P.dve, lambda e: e.reciprocal(out=o, in_=i), [in_], [out])

        def memset(out, val, eng="dve"):
            E = P.dve if eng == "dve" else P.pool
            o = out.ap
            P.emit(E, lambda e: e.memset(o, val), [], [out])

        def pe(mms, final=True):
            reads, writes, seen = [], [], set()
            for (o, l, r, s0, s1) in mms:
                reads += [l, r]
                if id(o.root) not in seen or True:
                    writes.append(o)
            items = [(o.ap, l.ap, r.ap, s0, s1) for (o, l, r, s0, s1) in mms]

            def fn(e):
                ins = None
                for (o, l, r, s0, s1) in items:
                    ins = e.matmul(o, l, r, start=s0, stop=s1)
                return ins
            P.emit(P.pe, fn, reads, writes, final=True)

        rot = {"i": 0, "set": [0, 1, 2, 3, 4, 6, 7]}

        def nb():
            b = rot["set"][rot["i"] % len(rot["set"])]
            rot["i"] += 1
            return psb[b]

        SSB = psb[5]
        cnt = {"r": 0, "t": 0, "o": 0, "p": 0, "w": 0, "wd": 0, "y": 0, "c": 0, "s": 0, "d": 0, "ob": 0}

        def nxt(key, lst):
            v = lst[cnt[key] % len(lst)]
            cnt[key] += 1
            return v

        def spv(l, a, b=None):
            b = a + 1 if b is None else b
            return smallp[:, l * NSP + a:l * NSP + b]

        def rstd_from(ssv, D):
            sqv = nxt("s", sqrtT)
            act(sqv[:, :], ssv, AF.Ln, bias=epsT[:, 0:1], scale=1.0 / D)
            rv = nxt("r", rstd)
            act(rv[:, :], sqv[:, :], AF.Exp, scale=-0.5)
            return rv

        dma("sp", smallp.all(), D_in.v(smallp_d))
        dma("sp", corep.all(), D_in.v(corep_d))
        dma("sp", condS.all(), D_in.v(condT))
        dma("sp", shiftm.all(), D_in.v(shiftm_d))
        dma("sp", dft256.all(), D_in.v(dft256_d))
        dma("sp", dftc.all(), D_in.v(dftc_d))
        memset(ones.all(), 1.0)
        memset(epsT.all(), EPS)
        for i in range(2):
            memset(rEO[i].all(), 0.0)
        act(scond.all(), condS.all(), AF.Silu)
        def mod_pieces(l, pieces):
            for piece in pieces:
                wb = nxt("w", wbuf)
                dma("pool", wb.all(), D_in.v(w_ada[l, :, piece * 512:(piece + 1) * 512].rearrange("(k p) c -> p k c", p=128)))
                b = nb()
                mms = []
                for mi in range(4):
                    for k in range(8):
                        mms.append((b[:, mi * 2:mi * 2 + 2], wb[:, k, mi * 128:(mi + 1) * 128], scond[:, k, 0:2], k == 0, k == 7))
                pe(mms)
                b3 = b.ap[:, 0:8].rearrange("p (a c) -> p a c", c=2)
                for c in range(2):
                    tt(modA[:, l, piece * 4:(piece + 1) * 4, c], View(b3[:, :, c], b.root, 0, 2048),
                       spv(l, 32 + piece * 4, 36 + piece * 4), ALU.add)
            if pieces and pieces[-1] == 11:
                for c in range(2):
                    stt(gs1[:, l, :, c], modA[:, l, 8:16, c], 1.0, spv(l, 0, 8), ALU.add, ALU.mult)
                    tt(gg1[:, l, :, c], modA[:, l, 16:24, c], spv(l, 8, 16), ALU.mult)
                    stt(gs2[:, l, :, c], modA[:, l, 32:40, c], 1.0, spv(l, 16, 24), ALU.add, ALU.mult)
                    tt(gg2[:, l, :, c], modA[:, l, 40:48, c], spv(l, 24, 32), ALU.mult)

        if "mod" in stages:
            mod_pieces(0, list(range(12)))

        def load_layer_small(l):
            dma("pool", spatT.all(), D_in.v(spatT_d[l]))
            dma("sp", spb.all(), D_in.v(spb_d[l]))
            dma("pool", wuq.all(), D_in.v(w_uq_d[l]))
            dma("pool", wukv.all(), D_in.v(w_ukv_d[l]))

        def big_norm(xb, gsT, shiftcol, l, c):
            for m in range(8):
                act(sq[:, m, :], xb[:, m, :], AF.Square)
            pe([(SSB[:, :], ones[:, :], sq[:, m, :], m == 0, m == 7) for m in range(8)])
            rv = rstd_from(SSB[:, :], 1024.0)
            for m in range(8):
                t = nxt("t", tmpf)
                stt(t[:, :], xb[:, m, :], gsT[:, l, m, c:c + 1], rv[:, :], ALU.mult, ALU.mult)
                act(H[:, m, :], t[:, :], AF.Identity, bias=modA[:, l, shiftcol + m, c:c + 1])

        def post_norm_residual(osb, xb, ggT, l, c):
            pe([(SSB[:, :], ones[:, :], sq[:, m, :], m == 0, m == 7) for m in range(8)])
            rv = rstd_from(SSB[:, :], 1024.0)
            for m in range(8):
                t = nxt("t", tmpf)
                stt(t[:, :], osb[:, m, :], ggT[:, l, m, c:c + 1], rv[:, :], ALU.mult, ALU.mult)
                tt(xb[:, m, :], xb[:, m, :], t[:, :], ALU.add)

        def in_proj(l, sample, t0, dst):
            def wunit(c0, n):
                wb = nxt("w", wbuf)
                dma("pool", wb[:, :, 0:n], D_in.v(w_in_p[l, :, c0:c0 + n].rearrange("(k p) c -> p k c", p=128)))
                return wb

            def fgroup(wb, ci, M=128):
                b = nb()
                pe([(b[0:M, :], wb[:, k, ci * 128:ci * 128 + M], H[:, k, :], k == 0, k == 7) for k in range(8)])
                return b
            wb = wunit(0, 512)
            for ci in range(2):
                b = fgroup(wb, ci)
                act(ublk[:, ci, :], b[:, :], AF.Copy)
            for ci in range(2):
                b = fgroup(wb, 2 + ci)
                act(hcp[ci][:, :], b[:, :], AF.Copy)
            if not sample and plevel < 1.1:
                return
            wb = wunit(512, 512)
            for ci in range(2):
                b = fgroup(wb, ci)
                if sample:
                    tt(zext[:, ci, 0:512], hcp[ci][:, :], b[:, :], ALU.mult)
                else:
                    zd = View(zext.ap[:, ci, 0:516].rearrange("p (s t) -> p s t", s=2)[:, :, 1:257], zext.root,
                              zext[:, ci, 0:516].lo, zext[:, ci, 0:516].hi)
                    hv = View(hcp[ci].ap.rearrange("p (s t) -> p s t", s=2), hcp[ci].root, 0, 2048)
                    bv = View(b.ap.rearrange("p (s t) -> p s t", s=2), b.root, 0, 2048)
                    tt(zd, hv, bv, ALU.mult)
            for ci in range(2):
                b = fgroup(wb, 2 + ci)
                act(gbblk[:, ci, :], b[:, :], AF.Copy)
            if not sample and plevel < 1.2:
                return
            wb = wunit(1024, 512)
            bq0 = fgroup(wb, 0)
            bq1 = fgroup(wb, 1)
            bkv = fgroup(wb, 2)
            bkr = fgroup(wb, 3)
            act(sqq[:, 0, :], bq0[:, :], AF.Square)
            act(sqq[:, 1, :], bq1[:, :], AF.Square)
            act(sqq[:, 2, :], bkv[:, :], AF.Square)
            pe([(SSB[:, :], ones[:, :], sqq[:, 0, :], True, False), (SSB[:, :], ones[:, :], sqq[:, 1, :], False, True)])
            rv = rstd_from(SSB[:, :], 192.0)
            stt(cqn[:, 0, :], bq0[:, :], spv(l, 88), rv[:, :], ALU.mult, ALU.mult)
            stt(cqn[:, 1, :], bq1[:, :], spv(l, 89), rv[:, :], ALU.mult, ALU.mult)
            pe([(SSB[:, :], ones[:, :], sqq[:, 2, :], True, True)])
            rv2 = rstd_from(SSB[:, :], 128.0)
            if sample:
                stt(ckvb[:, :], bkv[:, :], spv(l, 90), rv2[:, :], ALU.mult, ALU.mult)
                dma("sp", D_xk_in.v(xk_in.ap()[0:128, t0:t0 + 512]), ckvb[:, :])
                wb3 = wunit(1536, 128)
                bkp = fgroup(wb3, 0)
                t1, t2 = nxt("t", tmpf), nxt("t", tmpf)
                tt(t1[64:96, :], bkr[64:96, :], ropeT[64:96, 0, t0:t0 + 512], ALU.mult)
                tt(t2[64:96, :], bkp[64:96, :], ropeT[64:96, 1, t0:t0 + 512], ALU.mult)
                tt(krst[64:96, :], t1[64:96, :], t2[64:96, :], ALU.add)
                dma("sp", D_xk_in.v(xk_in.ap()[128:160, t0:t0 + 512]), krst[64:96, :])
            else:
                stt(ckvf[:, :], bkv[:, :], spv(l, 90), rv2[:, :], ALU.mult, ALU.mult)
                dma("sp", D_out.fresh(ckv_o[l, :, t0:t0 + 512]), ckvf[:, :])
                cp(ckv_all[:, 0:512], ckvf[:, :])
                act(krf[64:96, :], bkr[64:96, :], AF.Copy)
                dma("sp", D_out.fresh(kr_o[l, :, t0:t0 + 512]), krf[64:96, :])
                cp(krst[64:96, :], krf[64:96, :])
            if not sample and plevel < 1.3:
                return
            for h in range(4):
                bq = nb()
                pe([(bq[:, :], wuq[:, 0, 0, h * 128:(h + 1) * 128], cqn[:, 0, :], True, False),
                    (bq[:, :], wuq[:, 0, 1, h * 128:(h + 1) * 128], cqn[:, 1, :], False, True)])
                if sample:
                    ts(Qblk[:, h, :], bq[:, :], 1.0, ALU.mult)
                else:
                    act(Qblk[:, h, :], bq[:, :], AF.Copy)
                if sample:
                    bp = nb()
                    pe([(bp[:, :], wuq[:, 1, 0, h * 128:(h + 1) * 128], cqn[:, 0, :], True, False),
                        (bp[:, :], wuq[:, 1, 1, h * 128:(h + 1) * 128], cqn[:, 1, :], False, True)])
                    t1, t2 = nxt("t", tmpf), nxt("t", tmpf)
                    tt(t1[64:96, :], bq[64:96, :], ropeT[64:96, 0, t0:t0 + 512], ALU.mult)
                    tt(t2[64:96, :], bp[64:96, :], ropeT[64:96, 1, t0:t0 + 512], ALU.mult)
                    tt(Qblk[64:96, h, :], t1[64:96, :], t2[64:96, :], ALU.add)
            if not sample and plevel < 1.4:
                return
            wb = wunit(1664, 512)
            for ti in range(4):
                b = nb()
                pe([(b[:, :], H[:, k, ti * 128:(ti + 1) * 128], wb[:, k, :], k == 0, k == 7) for k in range(8)])
                act(vptok[:, ti, :], b[:, :], AF.Copy)

        def chunk_mlp():
            for ci in range(4):
                b = nb()
                mms = []
                for pr in range(2):
                    for gi in range(2):
                        q0 = (pr * 2 + gi) * 128
                        mms.append((b[:, q0:q0 + 128], vptok[:, ci, pr * 128:(pr + 1) * 128], spatT[:, pr * 2 + gi, :], True, True))
                pe(mms)
                b4 = b.ap.rearrange("p (a g q) -> p a g q", a=2, g=2)
                for gi in range(2):
                    r0, r1 = gi * 64, gi * 64 + 64
                    t = nxt("t", tmpf)
                    tv = View(t.ap[r0:r1, 0:256].rearrange("p (a q) -> p a q", a=2), t.root, 0, 1024)
                    tt(tv, View(b4[r0:r1, :, gi, :], b.root, 0, 2048), spb[r0:r1, :, :], ALU.add)
                    tt(mixblk[r0:r1, 0:2, ci * 128:(ci + 1) * 128], tv, ublk[r0:r1, 0:2, ci * 128:(ci + 1) * 128], ALU.mult)

        def conv_seg(l, zsrc, ch, gbv, outv, L):
            y = nxt("c", convy)
            a = zsrc
            act(y[:, 0:L], zext[:, ch, a + 1:a + 1 + L], AF.Identity, bias=spv(l, 86 + ch), scale=spv(l, 80 + 2 + ch))
            stt(y[:, 0:L], zext[:, ch, a:a + L], spv(l, 80 + ch), y[:, 0:L], ALU.mult, ALU.add)
            stt(y[:, 0:L], zext[:, ch, a + 2:a + 2 + L], spv(l, 80 + 4 + ch), y[:, 0:L], ALU.mult, ALU.add)
            tt(outv, gbv, y[:, 0:L], ALU.mult)

        def attn_norm(ob, hh, outv_fn):
            osb_ = nxt("o", Osb)
            act(osb_[:, :], ob[:, :], AF.Copy)
            r = rEO[hh]
            d0, d1 = (64, 128) if hh == 0 else (0, 64)
            o0, o1 = (0, 64) if hh == 0 else (64, 128)
            recip(r[d0:d1, :], osb_[d0:d1, :])
            pe([(SSB[:, :], shiftm[:, :], r[:, :], True, True)])
            tt(outv_fn(o0, o1), osb_[o0:o1, :], SSB[o0:o1, :], ALU.mult)

        def ones_V2():
            for i in range(2):
                memset(V2[i].all(), 1.0)

        def build_V(h, hh, nkt):
            vs = 0 if hh == 0 else 64
            for k0 in range(0, nkt, 8):
                n = min(8, nkt - k0)
                b = nb()
                pe([(b[:, j * 64:(j + 1) * 64], ckv_all[:, (k0 + j) * 128:(k0 + j + 1) * 128],
                     wukv[:, 512 + h * 64:512 + (h + 1) * 64], True, True) for j in range(n)])
                bv = View(b.ap[:, 0:n * 64].rearrange("p (j v) -> p j v", v=64), b.root, 0, 2048)
                ts(V2[hh][:, k0:k0 + n, vs:vs + 64], bv, 1.0, ALU.mult)

        def build_Knope(h, hh, ncols):
            for c0 in range(0, ncols, 512):
                n = min(512, ncols - c0)
                b = nb()
                pe([(b[:, 0:n], wukv[:, h * 128:(h + 1) * 128], ckv_all[:, c0:c0 + n], True, True)])
                ts(K2[hh][:, c0:c0 + n], b[:, 0:n], 1.0, ALU.mult)

        def ffn_and_rest(l, c, xb, dst_store):
            big_norm(xb, gs2, 24, l, c)
            for unit in range(11):
                wb = nxt("w", wbuf)
                dma("pool", wb.all(), D_in.v(w_gu_p[l, :, unit * 512:(unit + 1) * 512].rearrange("(k p) c -> p k c", p=128)))
                for jj in range(2):
                    j = unit * 2 + jj
                    bg, bu = nb(), nb()
                    pe([(bg[:, :], wb[:, k, jj * 256:jj * 256 + 128], H[:, k, :], k == 0, k == 7) for k in range(8)])
                    pe([(bu[:, :], wb[:, k, jj * 256 + 128:jj * 256 + 256], H[:, k, :], k == 0, k == 7) for k in range(8)])
                    s_ = nxt("d", sg)
                    act(s_[:, :], bg[:, :], AF.Silu)
                    tt(Affn[:, j, :], s_[:, :], bu[:, :], ALU.mult)
            osb = f32big[1]
            for du in range(4):
                wd = nxt("wd", wdbuf)
                dma("pool", wd.all(), D_in.v(w_down[l, :, du * 256:(du + 1) * 256].rearrange("(j p) c -> p j c", p=128)))
                for mi in range(2):
                    m = du * 2 + mi
                    b = nb()
                    pe([(b[:, :], wd[:, j, mi * 128:(mi + 1) * 128], Affn[:, j, :], j == 0, j == 21) for j in range(22)])
                    act(osb[:, m, :], b[:, :], AF.Copy)
                    act(sq[:, m, :], b[:, :], AF.Square)
            post_norm_residual(osb, xb, gg2, l, c)
            for m in range(8):
                dma("sp", View(dst_store.ap[:, m, :], dst_store.root, 0, 1), xb[:, m, :])

        def out_proj_residual(l, c, xb):
            osb = f32big[1]
            for wu in range(2):
                wb = nxt("w", wbuf)
                dma("pool", wb.all(), D_in.v(w_out[l, :, wu * 512:(wu + 1) * 512].rearrange("(k p) c -> p k c", p=128)))
                for mi in range(4):
                    m = wu * 4 + mi
                    b = nb()
                    pe([(b[:, :], wb[:, k, mi * 128:(mi + 1) * 128], mixblk[:, k, :], k == 0, k == 7) for k in range(8)])
                    act(osb[:, m, :], b[:, :], AF.Copy)
                    act(sq[:, m, :], b[:, :], AF.Square)
            post_norm_residual(osb, xb, gg1, l, c)

        def load_x(xb, src):
            for m in range(8):
                dma("sp", xb[:, m, :], View(src.ap[:, m, :], src.root, 0, 1))

        pref = {}

        def get_x(xb, l, sample, t0):
            if pref.pop(("x", l, sample, t0), None):
                return
            load_x(xb, xsrc(l, sample, t0))

        def prefetch_x(xb, l, sample, t0):
            load_x(xb, xsrc(l, sample, t0))
            pref[("x", l, sample, t0)] = True

        def xsrc(l, sample, t0):
            if l == 0:
                base = xT_s if sample else xT_p
                return D_in.v(base[:, :, t0:t0 + 512].rearrange("c p t -> p c t"))
            D = D_xs_s if sample else D_xs_p
            return D.v(D.ap[:, :, t0:t0 + 512].rearrange("c p t -> p c t"))

        def xdst(l, sample, t0):
            if l == depth - 1:
                base = ys_o if sample else yp_o
                return D_out.fresh(base[:, :, t0:t0 + 512].rearrange("c p t -> p c t"))
            D = D_xs_s if sample else D_xs_p
            return D.v(D.ap[:, :, t0:t0 + 512].rearrange("c p t -> p c t"))

        def prompt_block(l, blk):
            t0 = blk * 512
            xb = f32big[0]
            get_x(xb, l, False, t0)
            big_norm(xb, gs1, 0, l, 0)
            if plevel < 0.5:
                return
            in_proj(l, False, t0, None)
            if plevel < 2:
                return
            chunk_mlp()
            for s in range(2):
                for ch in range(2):
                    conv_seg(l, s * 258, ch, gbblk[:, ch, s * 256:(s + 1) * 256], mixblk[:, 2 + ch, s * 256:(s + 1) * 256], 256)
            if plevel < 3:
                return
            for s in range(2):
                bu, bv = nb(), nb()
                for tb, bb in ((0, bu), (1, bv)):
                    mms = []
                    for pr in range(2):
                        for ti in range(2):
                            mms.append((bb[:, pr * 256:(pr + 1) * 256], vptok[:, s * 2 + ti, 256 + pr * 128:256 + (pr + 1) * 128],
                                        dft256[:, tb, ti, :], ti == 0, ti == 1))
                    pe(mms)
                act(UVsb[:, 0, :], bu[:, :], AF.Copy)
                ts(UVsb[:, 1, :], bv[:, :], 1.0, ALU.mult)
                by = nb()
                mms = []
                for pr in range(2):
                    mms.append((by[:, pr * 256:(pr + 1) * 256], dftc[:, 0, :], UVsb[:, 0, pr * 256:(pr + 1) * 256], True, False))
                    mms.append((by[:, pr * 256:(pr + 1) * 256], dftc[:, 1, :], UVsb[:, 1, pr * 256:(pr + 1) * 256], False, True))
                pe(mms)
                byv = View(by.ap.rearrange("p (a m) -> p a m", a=2), by.root, 0, 2048)
                act(mixblk[:, 4:6, s * 256:(s + 1) * 256], byv, AF.Copy)
            if plevel < 4:
                return
            rot["set"] = [0, 1, 2, 3, 4]
            ones_V2()
            for h in range(4):
                hh = h % 2
                build_Knope(h, hh, 512)
                cp(K2[hh][64:96, 0:512], krst[64:96, :])
                build_V(h, hh, 4)
                ob = psb[6 + (cnt["ob"] % 2)]
                cnt["ob"] += 1
                for s in range(2):
                    for k2 in range(2):
                        kt = s * 2 + k2
                        bs = nb()
                        pe([(bs[:, 0:256], K2[hh][:, kt * 128:(kt + 1) * 128], Qblk[:, h, s * 256:(s + 1) * 256], True, True)])
                        pt = nxt("p", Pt)
                        act(pt[:, 0:256], bs[:, 0:256], AF.Exp, scale=ATT_SCALE)
                        pe([(ob[:, s * 256:(s + 1) * 256], V2[hh][:, kt, :], pt[:, 0:256], k2 == 0, k2 == 1)],
                           final=(s == 1 and k2 == 1))
                attn_norm(ob, hh, lambda o0, o1, h=h: mixblk[o0:o1, 6 + h // 2, :])
            rot["set"] = [0, 1, 2, 3, 4, 6, 7]
            if plevel < 5:
                return
            out_proj_residual(l, 0, xb)
            if plevel < 6:
                return
            ffn_and_rest(l, 0, xb, xdst(l, False, t0))

        def sample_A_block(l, blk):
            t0 = blk * 512
            if "mod" in stages and l + 1 < depth:
                mod_pieces(l + 1, list(range(blk * 3, blk * 3 + 3)))
            xb = f32big[0]
            get_x(xb, l, True, t0)
            big_norm(xb, gs1, 0, l, 1)
            if blk < 3:
                prefetch_x(xb, l, True, t0 + 512)
            elif "P" in stages:
                prefetch_x(xb, l, False, 0)
            in_proj(l, True, t0, None)
            chunk_mlp()
            dma("sp", D_mixS.v(mixS.ap()[0:2, :, t0:t0 + 512].rearrange("c p t -> p c t")), mixblk[:, 0:2, :])
            dma("sp", D_zS.v(zS.ap()[:, :, 1 + t0:1 + t0 + 512].rearrange("c p t -> p c t")), zext[:, :, 0:512])
            dma("sp", D_gbS.v(gbS.ap()[:, :, t0:t0 + 512].rearrange("c p t -> p c t")), gbblk[:, :, :])
            dma("sp", D_QS.v(QS.ap()[:, :, t0:t0 + 512].rearrange("h p t -> p h t")), Qblk[:, :, :])
            dma("sp", D_pc_in.v(pc_in.ap()[t0:t0 + 512, :].rearrange("(a p) c -> p a c", p=128)), vptok[:, :, 256:512])
            if blk == 0:
                for ch in range(2):
                    dma("sp", D_xk_in.v(xk_in.ap()[160 + ch, :].rearrange("(p c) -> p c", c=16)), zext[:, ch, 0:16])
            if blk == 3:
                for ch in range(2):
                    dma("sp", D_xk_in.v(xk_in.ap()[162 + ch, :].rearrange("(p c) -> p c", c=16)), zext[:, ch, 496:512])

        def exchange():
            a_in, a_out = xk_in.ap().opt(), xk_out.ap().opt()
            P.emit(P.pool, lambda e: e.collective_compute("AllGather", ALU.bypass, replica_groups=groups,
                                                          ins=[a_in], outs=[a_out]),
                   [D_xk_in.v()], [D_xk_out.v()], kind="cc")
            b_in, b_out = pc_in.ap().opt(), pc_out.ap().opt()
            P.emit(P.pool, lambda e: e.collective_compute("AllGather", ALU.bypass, replica_groups=groups,
                                                          ins=[b_in], outs=[b_out]),
                   [D_pc_in.v()], [D_pc_out.v()], kind="cc")

        def sample_mix(l):
            xo = xk_out.ap()
            for ch in range(2):
                dma("sp", hal[:, ch, 0, :], D_xk_out.v(xo[162 + ch, :].rearrange("(p c) -> p c", c=16)))
                dma("sp", hal[:, ch, 1, :], D_xk_out.v(xo[164 + 160 + ch, :].rearrange("(p c) -> p c", c=16)))
            ts(halm[:, :, 0], hal[:, :, 0, 15], corep[:, 0:1], ALU.mult)
            ts(halm[:, :, 1], hal[:, :, 1, 0], corep[:, 1:2], ALU.mult)
            def conv_block(blk):
                t0 = blk * 512
                if blk == 0:
                    dma("sp", zext[:, :, 1:514], D_zS.v(zS.ap()[:, :, 1:514].rearrange("c p t -> p c t")))
                    for ch in range(2):
                        cp(zext[:, ch, 0:1], halm[:, ch, 0:1])
                elif blk == 3:
                    dma("sp", zext[:, :, 0:513], D_zS.v(zS.ap()[:, :, t0:t0 + 513].rearrange("c p t -> p c t")))
                    for ch in range(2):
                        cp(zext[:, ch, 513:514], halm[:, ch, 1:2])
                else:
                    dma("sp", zext[:, :, 0:514], D_zS.v(zS.ap()[:, :, t0:t0 + 514].rearrange("c p t -> p c t")))
                dma("sp", gbblk[:, :, :], D_gbS.v(gbS.ap()[:, :, t0:t0 + 512].rearrange("c p t -> p c t")))
                ym = nxt("y", ymix)
                for ch in range(2):
                    conv_seg(l, 0, ch, gbblk[:, ch, :], ym[:, ch, :], 512)
                dma("sp", D_mixS.v(mixS.ap()[2:4, :, t0:t0 + 512].rearrange("c p t -> p c t")), ym[:, :, :])
            dma("sp", pc_all.all(), D_pc_out.v(pc_out.ap().rearrange("(a p) c -> p a c", p=128)))
            rot["set"] = [0, 1, 2, 3]
            acc = [psb[4], psb[5], psb[6], psb[7]]
            for mb in range(4):
                for ng in range(4):
                    ct, st_ = nxt("d", dftbuf), nxt("d", dftbuf)
                    for tb, buf in ((0, ct), (1, st_)):
                        dma("pool", buf.all(), D_in.v(dft4096_d[tb, ng * 1024:(ng + 1) * 1024, mb * 512:(mb + 1) * 512]
                                                     .rearrange("(a p) m -> p a m", p=128)))
                    mms = []
                    for a in range(8):
                        nt = ng * 8 + a
                        for pr in range(2):
                            mms.append((acc[pr][:, :], pc_all[:, nt, pr * 128:(pr + 1) * 128], ct[:, a, :], nt == 0, nt == 31))
                            mms.append((acc[2 + pr][:, :], pc_all[:, nt, pr * 128:(pr + 1) * 128], st_[:, a, :], nt == 0, nt == 31))
                    pe(mms, final=(ng == 3))
                for i in range(4):
                    if i % 2 == 0:
                        act(UVsb[:, i, :], acc[i][:, :], AF.Copy)
                    else:
                        ts(UVsb[:, i, :], acc[i][:, :], 1.0, ALU.mult)
                ym = nxt("y", ymix)
                for pr in range(2):
                    by = nb()
                    pe([(by[:, :], dftc[:, 0, :], UVsb[:, pr, :], True, False), (by[:, :], dftc[:, 1, :], UVsb[:, 2 + pr, :], False, True)])
                    act(ym[:, pr, :], by[:, :], AF.Copy)
                dma("sp", D_mixS.v(mixS.ap()[4:6, :, mb * 512:(mb + 1) * 512].rearrange("c p t -> p c t")), ym[:, :, :])
                conv_block(mb)
            rot["set"] = [0, 1, 2, 3, 4]
            ones_V2()
            dma("sp", ckv_all[:, 0:2048], D_xk_out.v(xo[0:128, :]))
            dma("sp", ckv_all[:, 2048:4096], D_xk_out.v(xo[164:292, :]))
            dma("pool", ckv_all[:, 4096:4352], D_in.v(cacheT_ckv[l]))
            for h in range(4):
                hh = h % 2
                build_Knope(h, hh, 4352)
                dma("sp", K2[hh][64:96, 0:2048], D_xk_out.v(xo[128:160, :]))
                dma("sp", K2[hh][64:96, 2048:4096], D_xk_out.v(xo[292:324, :]))
                dma("pool", K2[hh][64:96, 4096:4352], D_in.v(cacheT_kr[l]))
                build_V(h, hh, 34)
                qh = nxt("c", Qh)
                dma("sp", qh[:, :], D_QS.v(QS.ap()[h, :, :]))
                for qb in range(4):
                    ob = psb[6 + (cnt["ob"] % 2)]
                    cnt["ob"] += 1
                    pts = {}

                    def s_step(kt):
                        bs = nb()
                        pe([(bs[:, :], K2[hh][:, kt * 128:(kt + 1) * 128], qh[:, qb * 512:(qb + 1) * 512], True, True)])
                        pt = nxt("p", Pt)
                        act(pt[:, :], bs[:, :], AF.Exp, scale=ATT_SCALE)
                        pts[kt] = pt
                    s_step(0)
                    s_step(1)
                    s_step(2)
                    for kt in range(34):
                        if kt + 3 < 34:
                            s_step(kt + 3)
                        pe([(ob[:, :], V2[hh][:, kt, :], pts.pop(kt)[:, :], kt == 0, kt == 33)], final=(kt == 33))
                    ym = nxt("y", ymix)
                    attn_norm(ob, hh, lambda o0, o1: ym[o0:o1, 0, :])
                    o0, o1 = (0, 64) if hh == 0 else (64, 128)
                    dma("sp", D_mixS.v(mixS.ap()[6 + h // 2, o0:o1, qb * 512:(qb + 1) * 512]), ym[o0:o1, 0, :])
            rot["set"] = [0, 1, 2, 3, 4, 6, 7]

        def sample_F_block(l, blk):
            t0 = blk * 512
            xb = f32big[0]
            load_x(xb, xsrc(l, True, t0))
            if not pref.pop(("m", l, blk), None):
                dma("sp", mixblk.all(), D_mixS.v(mixS.ap()[:, :, t0:t0 + 512].rearrange("c p t -> p c t")))
            out_proj_residual(l, 1, xb)
            if blk < 3:
                dma("sp", mixblk.all(), D_mixS.v(mixS.ap()[:, :, t0 + 512:t0 + 1024].rearrange("c p t -> p c t")))
                pref[("m", l, blk + 1)] = True
            ffn_and_rest(l, 1, xb, xdst(l, True, t0))

        memset(zext.all(), 0.0)
        for l in range(depth):
            load_layer_small(l)
            dma("sp", ropeT.all(), D_in.v(ropeT_d.rearrange("a p t -> p a t")))
            if "SA" in stages:
                for blk in range(4):
                    sample_A_block(l, blk)
            memset(zext.all(), 0.0)
            if "P" in stages:
                prompt_block(l, 0)
            if "EX" in stages:
                exchange()
            if "P" in stages:
                prompt_block(l, 1)
            if "SM" in stages:
                sample_mix(l)
            if "SF" in stages:
                for blk in range(4):
                    sample_F_block(l, blk)

        fin = []
        for q in (P.sp, P.pool):
            for name, n in q.slots:
                if n > 0 and P.sp.waited.get(name, 0) < 16 * n:
                    fin.append((name, 16 * n))
        P.sp.ops.append((fin, None, None))

        semnames = ["pe", "act", "dve", "pool", "cc"] + [s_[0] for q in (P.sp, P.pool) for s_ in q.slots]
        sems = {n: st.enter_context(nc.semaphore(n)) for n in semnames}
        block = st.enter_context(nc.Block())

        def replay(eng):
            def run(e):
                for waits, fn, inc in eng.ops:
                    for k, v in waits:
                        e.wait_ge(sems[k], v)
                    if fn is None:
                        continue
                    ins = fn(e)
                    if inc is not None:
                        if inc[1] is None:
                            ins.then_inc(sems[inc[0]])
                        else:
                            ins.then_inc(sems[inc[0]], inc[1])
            return run

        block.tensor(replay(P.pe))
        block.scalar(replay(P.act))
        block.vector(replay(P.dve))
        block.gpsimd(replay(P.pool))
        block.sync(replay(P.sp))
    return nc


_CACHE = {}


def _consts():
    if "c" in _CACHE:
        return _CACHE["c"]
    bf = ml_dtypes.bfloat16
    n = np.arange(256)
    ang = 2 * np.pi * ((n[:, None] * n[None, :]) % 256) / 256.0
    C, S = np.cos(ang) / 16.0, np.sin(ang) / 16.0
    dft256 = np.stack([C, S], 0).reshape(2, 2, 128, 256).transpose(2, 0, 1, 3).astype(bf)
    ch = np.arange(64)
    angc = 2 * np.pi * ((ch[:, None] * ch[None, :]) % 64) / 64.0
    Cc, Sc = np.cos(angc) / 8.0, np.sin(angc) / 8.0
    CcB = np.zeros((128, 128)); nScB = np.zeros((128, 128))
    for g in range(2):
        CcB[g * 64:(g + 1) * 64, g * 64:(g + 1) * 64] = Cc
        nScB[g * 64:(g + 1) * 64, g * 64:(g + 1) * 64] = -Sc
    dftc = np.stack([CcB, nScB], 1).astype(bf)
    nn = np.arange(4096, dtype=np.int64)
    dft4096 = []
    for h in range(2):
        mm = h * 2048 + np.arange(2048, dtype=np.int64)
        k = (nn[:, None] * mm[None, :]) % 4096
        a = 2 * np.pi * k / 4096.0
        dft4096.append(np.stack([np.cos(a) / 64.0, np.sin(a) / 64.0], 0).astype(bf))
    shiftm = np.zeros((128, 128), np.float32)
    for m in range(128):
        shiftm[(m + 64) % 128, m] = 1.0
    rope = []
    inv = (10000.0 ** (-np.arange(0, 16, 2, dtype=np.float32) / 16.0)).astype(np.float32)
    for h in range(2):
        m = h * 2048 + np.arange(2048)
        pos = np.stack([(m // 64).astype(np.float32), (m % 64).astype(np.float32)], 0)
        tab = np.zeros((2, 128, 2048), np.float32)
        for r in range(32):
            a, b, j = r // 16, (r % 16) // 8, r % 8
            angr = (pos[a] * inv[j]).astype(np.float32)
            tab[0, 64 + r] = np.cos(angr)
            tab[1, 64 + r] = (-np.sin(angr)) if b == 0 else np.sin(angr)
        rope.append(tab)
    _CACHE["c"] = dict(dft256=dft256, dftc=dftc, dft4096=dft4096, shiftm=shiftm, rope=rope)
    return _CACHE["c"]


def kernel(x_prompt, x_sample, cache_ckv, cache_krope, c, c_ctx, w_ada, b_ada,
           g_pre_mix, g_post_mix, g_pre_ffn, g_post_ffn, w_in, spat_w, spat_b,
           conv_w, conv_b, g_q_lora, w_uq, g_kv_lora, w_ukv, w_out, w_gate_up, w_down):
    f = lambda a: np.ascontiguousarray(np.asarray(a, dtype=np.float32))
    x_prompt, x_sample, cache_ckv, cache_krope, c, c_ctx = map(f, (x_prompt, x_sample, cache_ckv, cache_krope, c, c_ctx))
    w_ada, b_ada, w_in, spat_w, spat_b, conv_w, conv_b = map(f, (w_ada, b_ada, w_in, spat_w, spat_b, conv_w, conv_b))
    g_pre_mix, g_post_mix, g_pre_ffn, g_post_ffn = map(f, (g_pre_mix, g_post_mix, g_pre_ffn, g_post_ffn))
    g_q_lora, w_uq, g_kv_lora, w_ukv, w_out, w_gate_up, w_down = map(f, (g_q_lora, w_uq, g_kv_lora, w_ukv, w_out, w_gate_up, w_down))
    K = _consts()
    L = DEPTH
    perm = np.array([(r // 16) * 16 + (1 - (r % 16) // 8) * 8 + r % 8 for r in range(32)])
    w_in_p = np.zeros((L, 1024, 2176), np.float32)
    w_in_p[:, :, 0:256] = w_in[:, :, 0:256]
    w_in_p[:, :, 256:512] = w_in[:, :, 512:768]
    w_in_p[:, :, 512:768] = w_in[:, :, 1024:1280]
    w_in_p[:, :, 768:1024] = w_in[:, :, 768:1024]
    w_in_p[:, :, 1024:1216] = w_in[:, :, 1536:1728]
    w_in_p[:, :, 1280:1408] = w_in[:, :, 1728:1856]
    w_in_p[:, :, 1408 + 64:1408 + 96] = w_in[:, :, 1856:1888]
    w_in_p[:, :, 1536 + 64:1536 + 96] = w_in[:, :, 1856 + perm]
    w_in_p[:, :, 1664:1920] = w_in[:, :, 256:512]
    w_in_p[:, :, 1920:2176] = w_in[:, :, 1280:1536]
    w_gu_p = np.ascontiguousarray(
        np.stack([w_gate_up[:, :, :2816].reshape(L, 1024, 22, 128), w_gate_up[:, :, 2816:].reshape(L, 1024, 22, 128)], 3)
        .reshape(L, 1024, 5632))
    spatT = np.ascontiguousarray(spat_w.transpose(0, 3, 1, 2))
    spb_bc = np.zeros((L, 128, 2, 128), np.float32)
    for pr in range(2):
        spb_bc[:, 0:64, pr, :] = spat_b[:, 2 * pr, None, :]
        spb_bc[:, 64:128, pr, :] = spat_b[:, 2 * pr + 1, None, :]
    w_uq_p = np.zeros((L, 128, 2, 2, 512), np.float32)
    for h in range(4):
        w_uq_p[:, :, 0, 0, h * 128:h * 128 + 96] = w_uq[:, 0:128, h * 96:(h + 1) * 96]
        w_uq_p[:, 0:64, 0, 1, h * 128:h * 128 + 96] = w_uq[:, 128:192, h * 96:(h + 1) * 96]
        src = h * 96 + 64 + perm
        w_uq_p[:, :, 1, 0, h * 128 + 64:h * 128 + 96] = w_uq[:, 0:128, src]
        w_uq_p[:, 0:64, 1, 1, h * 128 + 64:h * 128 + 96] = w_uq[:, 128:192, src]
    w_ukv_p = np.zeros((L, 128, 768), np.float32)
    _kv = w_ukv.reshape(L, 128, 4, 2, 64)
    for h in range(4):
        w_ukv_p[:, :, h * 128:h * 128 + 64] = _kv[:, :, h, 0, :]
        w_ukv_p[:, :, 512 + h * 64:512 + (h + 1) * 64] = _kv[:, :, h, 1, :]
    smallp = np.zeros((128, L * NSP), np.float32)
    v8 = lambda v: v.reshape(8, 128).T
    for l in range(L):
        o = l * NSP
        smallp[:, o + 0:o + 8] = v8(g_pre_mix[l])
        smallp[:, o + 8:o + 16] = v8(g_post_mix[l])
        smallp[:, o + 16:o + 24] = v8(g_pre_ffn[l])
        smallp[:, o + 24:o + 32] = v8(g_post_ffn[l])
        smallp[:, o + 32:o + 80] = b_ada[l].reshape(48, 128).T
        for k in range(3):
            smallp[:, o + 80 + 2 * k:o + 82 + 2 * k] = conv_w[l, k].reshape(2, 128).T
        smallp[:, o + 86:o + 88] = conv_b[l].reshape(2, 128).T
        smallp[:, o + 88] = g_q_lora[l, 0:128]
        smallp[0:64, o + 89] = g_q_lora[l, 128:192]
        smallp[:, o + 90] = g_kv_lora[l]
    shared = dict(w_ada=w_ada, w_in_p=w_in_p, w_out=w_out, w_gu_p=w_gu_p, w_down=w_down, spatT=spatT, spb_bc=spb_bc,
                  w_uq_p=w_uq_p, w_ukv_p=w_ukv_p, smallp=smallp, dft256=K["dft256"], dftc=K["dftc"], shiftm=K["shiftm"])
    in_maps = []
    for i in range(8):
        b, h = i // 2, i % 2
        xp = x_prompt[4 * i:4 * i + 4].reshape(TP, 1024)
        xs = x_sample[b, h * TS:(h + 1) * TS]
        cond = np.stack([c_ctx, c[b]], -1)
        corep = np.zeros((128, 2), np.float32)
        corep[:, 0] = float(h)
        corep[:, 1] = float(1 - h)
        m = dict(shared)
        m.update(
            xT_p=np.ascontiguousarray(xp.T.reshape(8, 128, TP)),
            xT_s=np.ascontiguousarray(xs.T.reshape(8, 128, TS)),
            condT=np.ascontiguousarray(cond.reshape(8, 128, 2).transpose(1, 0, 2)),
            cacheT_ckv=np.ascontiguousarray(cache_ckv[b].transpose(0, 2, 1)),
            cacheT_kr=np.ascontiguousarray(cache_krope[b].transpose(0, 2, 1)),
            corep=corep, ropeT=K["rope"][h], dft4096=K["dft4096"][h],
        )
        in_maps.append(m)
    if _CACHE.get("prep_only"):
        return in_maps
    if "nc" not in _CACHE:
        _CACHE["nc"] = build_program()
    res = run_bass_kernel_spmd(_CACHE["nc"], in_maps, core_ids=list(range(8)))
    return _post(res.results)


def _post(results):
    class _R:
        pass
    res = _R()
    res.results = results
    y_prompt = np.zeros((32, 256, 1024), np.float32)
    y_sample = np.zeros((4, 4096, 1024), np.float32)
    state_ckv = np.zeros((32, DEPTH, 256, 128), np.float32)
    state_kr = np.zeros((32, DEPTH, 256, 32), np.float32)
    for i in range(8):
        r = res.results[i]
        b, h = i // 2, i % 2
        y_prompt[4 * i:4 * i + 4] = np.asarray(r["yp_o"]).reshape(1024, TP).T.reshape(4, 256, 1024)
        y_sample[b, h * TS:(h + 1) * TS] = np.asarray(r["ys_o"]).reshape(1024, TS).T
        state_ckv[4 * i:4 * i + 4] = np.asarray(r["ckv_o"]).transpose(2, 0, 1).reshape(4, 256, DEPTH, 128).transpose(0, 2, 1, 3)
        state_kr[4 * i:4 * i + 4] = np.asarray(r["kr_o"]).transpose(2, 0, 1).reshape(4, 256, DEPTH, 32).transpose(0, 2, 1, 3)
    return (y_prompt, y_sample, state_ckv, state_kr)
```
